# Optimizing a Trainium2 kernel written in Bass

```python
import jax
import jax.numpy as jnp
from jax import lax
import numpy as np

D_MODEL = 2048
BATCH = 8
SEQ = 4096
DEPTH = 4

GRID_W = 64
CTX_LEN = 256
N_MIXERS = 2
N_ATTN_LAYERS = (DEPTH + 1) // 2
N_MLSTM_LAYERS = DEPTH // 2
RMS_EPS = 1e-6

ATTN_HEAD_DIM = 128
ATTN_Q_HEADS = D_MODEL // ATTN_HEAD_DIM
ATTN_KV_HEADS = 4
ATTN_GROUP = ATTN_Q_HEADS // ATTN_KV_HEADS
ATTN_QKV_COLS = (ATTN_Q_HEADS + 2 * ATTN_KV_HEADS) * ATTN_HEAD_DIM
WINDOW = 128
ATTN_BLOCK = WINDOW
ROPE_THETA = 10000.0
NEG_INF = -1e30

MLSTM_HEADS = 8
MLSTM_DV = D_MODEL // MLSTM_HEADS
MLSTM_DQK = MLSTM_DV // 2
MLSTM_CHUNK = 64
MLSTM_QK_COLS = MLSTM_HEADS * MLSTM_DQK
MLSTM_V_COLS = MLSTM_HEADS * MLSTM_DV
MLSTM_IN_COLS = 2 * MLSTM_QK_COLS + 2 * MLSTM_V_COLS + 4 * MLSTM_HEADS

D_FF = 5632
CONV_W = 3

kernel_name = 'hybrid_swa_mlstm_convffn_dit'


def rmsnorm(x, g):
    xf = x.astype(jnp.float32)
    y = xf * lax.rsqrt(jnp.mean(xf * xf, axis=-1, keepdims=True) + RMS_EPS)
    return (y * g.astype(jnp.float32)).astype(x.dtype)


def modulate(h, shift, scale):
    return h * (1.0 + scale) + shift


def axial_rope_tables(n_tokens):
    rows = n_tokens // GRID_W
    row = jnp.repeat(jnp.arange(rows, dtype=jnp.float32), GRID_W)
    col = jnp.tile(jnp.arange(GRID_W, dtype=jnp.float32), rows)
    n_freq = ATTN_HEAD_DIM // 4
    inv_freq = ROPE_THETA ** (-jnp.arange(n_freq, dtype=jnp.float32) / n_freq)
    ang = jnp.concatenate([row[:, None] * inv_freq, col[:, None] * inv_freq], axis=-1)
    ang = jnp.concatenate([ang, ang], axis=-1)
    return jnp.cos(ang), jnp.sin(ang)


def apply_rope(t, cos, sin):
    half = t.shape[-1] // 2
    tf = t.astype(jnp.float32)
    rot = jnp.concatenate([-tf[..., half:], tf[..., :half]], axis=-1)
    return (tf * cos[None, :, None, :] + rot * sin[None, :, None, :]).astype(t.dtype)


def windowed_gqa_with_context(hx, hc, w_qkv, sink, w_o, cos, sin, need_ctx_out):
    B, L, _ = hx.shape
    Lc = hc.shape[1]
    H, KVH, G, Dh, BLK = ATTN_Q_HEADS, ATTN_KV_HEADS, ATTN_GROUP, ATTN_HEAD_DIM, ATTN_BLOCK
    scale = Dh ** -0.5

    def project(h):
        T = h.shape[1]
        q, k, v = jnp.split(h @ w_qkv, [H * Dh, (H + KVH) * Dh], axis=-1)
        return q.reshape(B, T, H, Dh), k.reshape(B, T, KVH, Dh), v.reshape(B, T, KVH, Dh)

    qx, kx, vx = project(hx)
    qc, kc, vc = project(hc)
    qx = apply_rope(qx, cos, sin)
    kx = apply_rope(kx, cos, sin)
    sink_g = sink.reshape(KVH, G).astype(jnp.float32)

    nb = L // BLK
    qb = qx.reshape(B, nb, BLK, KVH, G, Dh) * scale

    def band(t):
        tp = jnp.pad(t, ((0, 0), (BLK, BLK), (0, 0), (0, 0)))
        tb = tp.reshape(B, nb + 2, BLK, KVH, Dh)
        return jnp.concatenate([tb[:, :-2], tb[:, 1:-1], tb[:, 2:]], axis=2)

    kb, vb = band(kx), band(vx)
    qi = jnp.arange(BLK)[:, None]
    kj = jnp.arange(3 * BLK)[None, :]
    kpos = (jnp.arange(nb)[:, None, None] - 1) * BLK + kj[None]
    valid = (jnp.abs(kj - BLK - qi) <= WINDOW)[None] & (kpos >= 0) & (kpos < L)

    s_loc = jnp.einsum('bnqhgd,bnkhd->bhgnqk', qb, kb).astype(jnp.float32)
    s_loc = jnp.where(valid, s_loc, NEG_INF)
    s_ctx = jnp.einsum('bnqhgd,bchd->bhgnqc', qb, kc).astype(jnp.float32)
    s_sink = jnp.broadcast_to(sink_g[None, :, :, None, None, None], s_loc.shape[:-1] + (1,))
    p = jax.nn.softmax(jnp.concatenate([s_loc, s_ctx, s_sink], axis=-1), axis=-1).astype(vx.dtype)
    ox = (jnp.einsum('bhgnqk,bnkhd->bnqhgd', p[..., :3 * BLK], vb)
          + jnp.einsum('bhgnqc,bchd->bnqhgd', p[..., 3 * BLK:3 * BLK + Lc], vc))
    out_x = ox.reshape(B, L, H * Dh) @ w_o
    if not need_ctx_out:
        return out_x, None

    qcg = qc.reshape(B, Lc, KVH, G, Dh) * scale
    s_cc = jnp.einsum('bqhgd,bkhd->bhgqk', qcg, kc).astype(jnp.float32)
    s_sink_c = jnp.broadcast_to(sink_g[None, :, :, None, None], s_cc.shape[:-1] + (1,))
    pc = jax.nn.softmax(jnp.concatenate([s_cc, s_sink_c], axis=-1), axis=-1)[..., :Lc].astype(vc.dtype)
    oc = jnp.einsum('bhgqk,bkhd->bqhgd', pc, vc)
    out_c = oc.reshape(B, Lc, H * Dh) @ w_o
    return out_x, out_c


def mlstm_chunk_scan(q, k, v, i_pre, log_f, state):
    B, NH, T, _ = q.shape
    CH = MLSTM_CHUNK
    nc = T // CH

    def chunks(t):
        t = t.reshape(t.shape[:2] + (nc, CH) + t.shape[3:])
        return jnp.moveaxis(t, 2, 0)

    causal = jnp.tril(jnp.ones((CH, CH), dtype=bool))

    def step(carry, inp):
        C, n, m = carry
        qc, kc, vc, ic, fc = inp
        b = jnp.cumsum(fc, axis=-1)
        d = b[..., :, None] - b[..., None, :] + ic[..., None, :]
        d = jnp.where(causal, d, -jnp.inf)
        inter = b + m[..., None]
        m_t = jnp.maximum(inter, jnp.max(d, axis=-1))
        w = jnp.exp(d - m_t[..., None])
        e_inter = jnp.exp(inter - m_t)
        s = jnp.einsum('bhtd,bhsd->bhts', qc, kc) * w
        num = e_inter[..., None] * jnp.einsum('bhtd,bhdv->bhtv', qc, C) + jnp.einsum('bhts,bhsv->bhtv', s, vc)
        den = e_inter * jnp.einsum('bhtd,bhd->bht', qc, n) + jnp.sum(s, axis=-1)
        h = num / jnp.maximum(jnp.abs(den), jnp.exp(-m_t))[..., None]
        b_end = b[..., -1]
        g = b_end[..., None] - b + ic
        m_new = jnp.maximum(b_end + m, jnp.max(g, axis=-1))
        e_prev = jnp.exp(b_end + m - m_new)
        wg = jnp.exp(g - m_new[..., None])
        C_new = e_prev[..., None, None] * C + jnp.einsum('bhs,bhsd,bhsv->bhdv', wg, kc, vc)
        n_new = e_prev[..., None] * n + jnp.einsum('bhs,bhsd->bhd', wg, kc)
        return (C_new, n_new, m_new), h

    state, h = lax.scan(step, state, (chunks(q), chunks(k), chunks(v), chunks(i_pre), chunks(log_f)))
    h = jnp.moveaxis(h, 0, 2).reshape(B, NH, T, MLSTM_DV)
    return h, state


def mlstm_project(h, w_in, b_in):
    B, T, _ = h.shape
    NH = MLSTM_HEADS
    z = (h @ w_in + b_in).astype(jnp.float32)
    q, k, v, o, g = jnp.split(z, [MLSTM_QK_COLS, 2 * MLSTM_QK_COLS, 2 * MLSTM_QK_COLS + MLSTM_V_COLS,
                                  2 * MLSTM_QK_COLS + 2 * MLSTM_V_COLS], axis=-1)
    q = q.reshape(B, T, NH, MLSTM_DQK).transpose(0, 2, 1, 3)
    k = k.reshape(B, T, NH, MLSTM_DQK).transpose(0, 2, 1, 3) * (MLSTM_DQK ** -0.5)
    v = v.reshape(B, T, NH, MLSTM_DV).transpose(0, 2, 1, 3)
    g = g.reshape(B, T, 4, NH).transpose(2, 0, 3, 1)
    fwd = (g[0], jax.nn.log_sigmoid(g[1]))
    bwd = (g[2], jax.nn.log_sigmoid(g[3]))
    return q, k, v, o, fwd, bwd


def mlstm_output(h, o, g_head, w_o, dtype):
    B, NH, T, DV = h.shape
    hn = h * lax.rsqrt(jnp.mean(h * h, axis=-1, keepdims=True) + RMS_EPS)
    hn = hn.transpose(0, 2, 1, 3).reshape(B, T, NH * DV)
    y = jax.nn.sigmoid(o) * hn * g_head.astype(jnp.float32)
    return y.astype(dtype) @ w_o


def bidirectional_mlstm_with_context(hx, hc, w_in, b_in, g_head, w_o, need_ctx_out):
    B = hx.shape[0]
    qx, kx, vx, ox, fx, bx = mlstm_project(hx, w_in, b_in)
    qc, kc, vc, oc, fc, bc = mlstm_project(hc, w_in, b_in)
    zero = (jnp.zeros((B, MLSTM_HEADS, MLSTM_DQK, MLSTM_DV), jnp.float32),
            jnp.zeros((B, MLSTM_HEADS, MLSTM_DQK), jnp.float32),
            jnp.zeros((B, MLSTM_HEADS), jnp.float32))

    def flip(t):
        return jnp.flip(t, axis=2)

    hc_f, st_f = mlstm_chunk_scan(qc, kc, vc, fc[0], fc[1], zero)
    hx_f, _ = mlstm_chunk_scan(qx, kx, vx, fx[0], fx[1], st_f)
    hc_b, st_b = mlstm_chunk_scan(flip(qc), flip(kc), flip(vc), flip(bc[0]), flip(bc[1]), zero)
    hx_b, _ = mlstm_chunk_scan(flip(qx), flip(kx), flip(vx), flip(bx[0]), flip(bx[1]), st_b)

    out_x = mlstm_output(hx_f + flip(hx_b), ox, g_head, w_o, hx.dtype)
    if not need_ctx_out:
        return out_x, None
    out_c = mlstm_output(hc_f + flip(hc_b), oc, g_head, w_o, hc.dtype)
    return out_x, out_c


def conv_ffn(h, w_up, conv_w, conv_b, w_down):
    T = h.shape[1]
    u = h @ w_up
    pad = CONV_W // 2
    up = jnp.pad(u, ((0, 0), (pad, pad), (0, 0)))
    acc = conv_b
    for j in range(CONV_W):
        acc = acc + up[:, j:j + T] * conv_w[j]
    a, b = jnp.split(acc, 2, axis=-1)
    return (jax.nn.silu(a) * b) @ w_down


def setup_inputs(seed: int = 0) -> dict:
    key = jax.random.key(seed)
    ks = jax.random.split(key, 20)
    f32 = jnp.float32
    D = D_MODEL
    NH = MLSTM_HEADS

    def normal(k, shape, scale=1.0):
        return scale * jax.random.normal(k, shape, f32)

    gate_offset = jnp.concatenate([
        jnp.zeros((2 * MLSTM_QK_COLS + 2 * MLSTM_V_COLS,), f32),
        jnp.full((NH,), -2.0, f32), jnp.linspace(3.0, 6.0, NH, dtype=f32),
        jnp.full((NH,), -2.0, f32), jnp.linspace(3.0, 6.0, NH, dtype=f32)])
    return {
        'x': normal(ks[0], (BATCH, SEQ, D)),
        'c': normal(ks[1], (BATCH, D)),
        'ctx': normal(ks[2], (BATCH, CTX_LEN, D)),
        'c_ctx': normal(ks[3], (D,)),
        'w_mod': normal(ks[4], (DEPTH, D, 6 * D), 0.5 * D ** -0.5),
        'b_mod': normal(ks[5], (DEPTH, 6 * D), 0.02),
        'g_mix': 1.0 + normal(ks[6], (DEPTH, D), 0.02),
        'g_ffn': 1.0 + normal(ks[7], (DEPTH, D), 0.02),
        'attn_w_qkv': normal(ks[8], (N_ATTN_LAYERS, D, ATTN_QKV_COLS), D ** -0.5),
        'attn_sink': normal(ks[9], (N_ATTN_LAYERS, ATTN_Q_HEADS)),
        'attn_w_o': normal(ks[10], (N_ATTN_LAYERS, ATTN_Q_HEADS * ATTN_HEAD_DIM, D), (ATTN_Q_HEADS * ATTN_HEAD_DIM) ** -0.5),
        'mlstm_w_in': normal(ks[11], (N_MLSTM_LAYERS, D, MLSTM_IN_COLS), D ** -0.5),
        'mlstm_b_in': gate_offset + normal(ks[12], (N_MLSTM_LAYERS, MLSTM_IN_COLS), 0.05),
        'mlstm_g_head': 1.0 + normal(ks[13], (N_MLSTM_LAYERS, MLSTM_V_COLS), 0.02),
        'mlstm_w_o': normal(ks[14], (N_MLSTM_LAYERS, MLSTM_V_COLS, D), MLSTM_V_COLS ** -0.5),
        'ffn_w_up': normal(ks[15], (DEPTH, D, 2 * D_FF), D ** -0.5),
        'ffn_conv_w': normal(ks[16], (DEPTH, CONV_W, 2 * D_FF), CONV_W ** -0.5),
        'ffn_conv_b': normal(ks[17], (DEPTH, 2 * D_FF), 0.02),
        'ffn_w_down': normal(ks[18], (DEPTH, D_FF, D), D_FF ** -0.5),
        'g_final': 1.0 + normal(ks[19], (D,), 0.02),
    }


def reference(x, c, ctx, c_ctx, w_mod, b_mod, g_mix, g_ffn, attn_w_qkv, attn_sink, attn_w_o,
              mlstm_w_in, mlstm_b_in, mlstm_g_head, mlstm_w_o, ffn_w_up, ffn_conv_w, ffn_conv_b,
              ffn_w_down, g_final):
    L = x.shape[1]
    cos, sin = axial_rope_tables(L)
    sc = jax.nn.silu(c)
    scc = jax.nn.silu(c_ctx)
    for i in range(DEPTH):
        last = i == DEPTH - 1
        j = i // N_MIXERS
        mx = (sc @ w_mod[i] + b_mod[i])[:, None, :]
        mc = scc @ w_mod[i] + b_mod[i]
        sh1x, sc1x, gt1x, sh2x, sc2x, gt2x = jnp.split(mx, 6, axis=-1)
        sh1c, sc1c, gt1c, sh2c, sc2c, gt2c = jnp.split(mc, 6, axis=-1)

        hx = modulate(rmsnorm(x, g_mix[i]), sh1x, sc1x)
        hc = modulate(rmsnorm(ctx, g_mix[i]), sh1c, sc1c)
        if i % N_MIXERS == 0:
            dx, dc = windowed_gqa_with_context(hx, hc, attn_w_qkv[j], attn_sink[j], attn_w_o[j],
                                               cos, sin, not last)
        else:
            dx, dc = bidirectional_mlstm_with_context(hx, hc, mlstm_w_in[j], mlstm_b_in[j],
                                                      mlstm_g_head[j], mlstm_w_o[j], not last)
        x = x + gt1x * dx
        hx = modulate(rmsnorm(x, g_ffn[i]), sh2x, sc2x)
        x = x + gt2x * conv_ffn(hx, ffn_w_up[i], ffn_conv_w[i], ffn_conv_b[i], ffn_w_down[i])
        if not last:
            ctx = ctx + gt1c * dc
            hc = modulate(rmsnorm(ctx, g_ffn[i]), sh2c, sc2c)
            ctx = ctx + gt2c * conv_ffn(hc, ffn_w_up[i], ffn_conv_w[i], ffn_conv_b[i], ffn_w_down[i])
    return rmsnorm(x, g_final)
```

```python
import numpy as np
from contextlib import ExitStack
import concourse.bass as bass
import concourse.mybir as mybir
from concourse.bass_utils import run_bass_kernel_spmd

F32 = mybir.dt.float32
BF16 = mybir.dt.bfloat16
AF = mybir.ActivationFunctionType
ALU = mybir.AluOpType

D = 2048
KC = 16
DFF = 5632
NJ = 44
CT = 256
EPS = 1e-6
BIG = 1 << 40


class Op:
    __slots__ = ("eng", "fn", "deps", "needed", "semval", "is_dma", "dsem", "dval")


class Sched:
    CE = ("pe", "act", "dve", "pool", "sp")

    NPOOL = 12

    def __init__(self, nc, ndma=28):
        self.nc = nc
        self.ops = {e: [] for e in self.CE}
        self.wr = {}
        self.rd = {}
        self.ndma = ndma
        self.dma_cnt = [0] * ndma
        self.dma_last = [None] * ndma
        self.dma_rr = 0
        self.dma_rr_pool = 0
        self.last = {e: None for e in self.CE}
        self.pending = {e: set() for e in self.CE}
        self.nops = 0

    @staticmethod
    def _norm(r):
        if isinstance(r, str):
            return (r, 0, BIG)
        return r

    def op(self, eng, fn, reads=(), writes=(), dma=False, extra=()):
        o = Op()
        o.eng = eng
        o.fn = fn
        o.needed = False
        o.semval = 0
        o.is_dma = dma
        o.dsem = -1
        o.dval = 0
        deps = set(extra)
        rds = [self._norm(r) for r in reads if r is not None]
        wrs = [self._norm(r) for r in writes if r is not None]
        wrs += [r for r in rds if r[0].startswith("ps")]
        rds = [r for r in rds if not r[0].startswith("ps")]
        for (name, lo, hi) in rds:
            for (l, h, w) in self.wr.get(name, ()):
                if l < hi and lo < h:
                    deps.add(w)
        for (name, lo, hi) in wrs:
            for (l, h, w) in self.wr.get(name, ()):
                if l < hi and lo < h:
                    deps.add(w)
            for (l, h, w) in self.rd.get(name, ()):
                if l < hi and lo < h:
                    deps.add(w)
        for (name, lo, hi) in rds:
            lst = self.rd.setdefault(name, [])
            if not dma:
                lst[:] = [t for t in lst if not (t[2].eng == eng and not t[2].is_dma and lo <= t[0] and t[1] <= hi)]
            lst.append((lo, hi, o))
        for (name, lo, hi) in wrs:
            lst = self.wr.setdefault(name, [])
            lst[:] = [t for t in lst if not (lo <= t[0] and t[1] <= hi)]
            lst.append((lo, hi, o))
            lst2 = self.rd.get(name)
            if lst2:
                lst2[:] = [t for t in lst2 if not (lo <= t[0] and t[1] <= hi)]
        deps |= self.pending[eng]
        self.pending[eng] = set()
        deps.discard(o)
        if eng == "pe" and not dma:
            deps = {d for d in deps if not (d.eng == "pe" and not d.is_dma)}
        for d in deps:
            d.needed = True
        o.deps = deps
        if dma:
            if eng == "pool":
                k = self.ndma - self.NPOOL + self.dma_rr_pool
                self.dma_rr_pool = (self.dma_rr_pool + 1) % self.NPOOL
            else:
                k = self.dma_rr
                self.dma_rr = (k + 1) % (self.ndma - self.NPOOL)
            if self.dma_last[k] is not None:
                self.dma_last[k].needed = True
                o.deps.add(self.dma_last[k])
            self.dma_cnt[k] += 1
            o.dsem = k
            o.dval = 16 * self.dma_cnt[k]
            self.dma_last[k] = o
        else:
            self.last[eng] = o
        self.ops[eng].append(o)
        self.nops += 1
        return o

    def barrier(self, final=False):
        deps = set()
        for e in self.CE:
            if self.last[e] is not None:
                deps.add(self.last[e])
        for k in range(self.ndma if final else self.ndma - self.NPOOL):
            if self.dma_last[k] is not None:
                deps.add(self.dma_last[k])
        for d in deps:
            d.needed = True
        for e in self.CE:
            self.pending[e] = set(deps)
        self.wr = {}
        self.rd = {}

    def emit(self, es):
        nc = self.nc
        csem = {e: es.enter_context(nc.semaphore("c_" + e)) for e in self.CE}
        dsem = [es.enter_context(nc.semaphore("d%d" % k)) for k in range(self.ndma)]
        self.barrier(final=True)
        finals = {e: self.pending[e] for e in self.CE}
        for e in self.CE:
            cnt = 0
            for o in self.ops[e]:
                if o.needed and not o.is_dma:
                    cnt += 1
                    o.semval = cnt

        self.stats = {e: (len(self.ops[e]), max([o.semval for o in self.ops[e]] + [0])) for e in self.CE}
        self.stats['dma'] = max(self.dma_cnt) * 16
        def run(ename, h):
            known = {}

            def waits(deps):
                need = {}
                for d in deps:
                    if d.is_dma:
                        key, val = ("d", d.dsem), d.dval
                    else:
                        key, val = ("c", d.eng), d.semval
                    if need.get(key, 0) < val:
                        need[key] = val
                for key, val in need.items():
                    if known.get(key, 0) < val:
                        sem = dsem[key[1]] if key[0] == "d" else csem[key[1]]
                        h.wait_ge(sem, val)
                        known[key] = val

            for o in self.ops[ename]:
                waits(o.deps)
                inst = o.fn(h)
                if o.is_dma:
                    inst.then_inc(dsem[o.dsem], 16)
                elif o.needed:
                    inst.then_inc(csem[ename], 1)
            waits(finals[ename])

        block = es.enter_context(nc.Block())

        @block.tensor
        def _(h):
            run("pe", h)

        @block.scalar
        def _(h):
            run("act", h)

        @block.vector
        def _(h):
            run("dve", h)

        @block.gpsimd
        def _(h):
            run("pool", h)

        @block.sync
        def _(h):
            run("sp", h)


def seq_tiles(L, nt, base):
    o = [((k * L) // nt) // 2 * 2 for k in range(nt)] + [L]
    tiles = []
    for k in range(nt):
        m_lo = 0 if k == 0 else o[k] + 1
        m_hi = L if k == nt - 1 else o[k + 1] + 1
        tiles.append(dict(o_lo=base + o[k], o_hi=base + o[k + 1], m_lo=base + m_lo, m_hi=base + m_hi,
                          first=(k == 0), last=(k == nt - 1)))
    return tiles


class Prog:
    def __init__(self, L, plan, depth):
        self.L = L
        self.T = CT + L
        self.plan = plan
        self.depth = depth
        self.nc = bass.Bass("TRN2", target_bir_lowering=False)
        self.s = Sched(self.nc)
        self.es = ExitStack()
        self.bg = {}

    def din(self, name, shape, dt=F32):
        return self.nc.dram_tensor(name, list(shape), dt, kind="ExternalInput")

    def dscr(self, name, shape, dt):
        return self.nc.dram_tensor(name, list(shape), dt, kind="Internal")

    def sb(self, stack, name, shape, dt):
        self._uid = getattr(self, "_uid", 0) + 1
        return stack.enter_context(self.nc.sbuf_tensor("sb%d_%s" % (self._uid, name), list(shape), dt))

    def dma(self, eng, out, in_, reads=(), writes=(), extra=(), **kw):
        return self.s.op(eng, lambda h: h.dma_start(out=out, in_=in_, **kw), reads=reads, writes=writes,
                         dma=True, extra=extra)

    def cast_weights(self, key, dst, src, rows, cols):
        n = rows * cols
        assert n % 1024 == 0
        dv = dst.rearrange("r c -> (r c)").rearrange("(a b) -> a b", b=1024)
        sv = src.rearrange("r c -> (r c)").rearrange("(a b) -> a b", b=1024)
        nr = n // 1024
        ops = []
        r = 0
        while r < nr:
            r1 = min(nr, r + 8192)
            ops.append(self.dma("pool", dv[r:r1, :], sv[r:r1, :]))
            r = r1
        self.bg[key] = ops

    def build(self):
        nc, s, es = self.nc, self.s, self.es
        L, T, depth = self.L, self.T, self.depth
        NB = T // 128
        self.x_in = self.din("x", [L, D])
        self.ctx_in = self.din("ctx", [CT, D])
        self.cc_in = self.din("cc", [128, KC, 2])
        self.wmod_in = self.din("w_mod", [depth, 24 * 128, KC * 512])
        self.bmod_in = self.din("b_mod", [depth, 128, 96, 2])
        self.gmix_in = self.din("g_mix", [depth, 128, KC, 2])
        self.gffn_in = self.din("g_ffn", [depth, 128, KC, 2])
        self.gfin_in = self.din("g_final", [128, KC])
        self.wup_in = self.din("w_up", [depth, NJ * 128, KC * 256])
        self.wdn_in = self.din("w_down", [depth, 16 * 128, NJ * 128])
        self.cw_in = self.din("conv_w", [depth, 128, 88, 3])
        self.cb_in = self.din("conv_b", [depth, 128, 88])
        self.ident_in = self.din("ident", [128, 128])
        na = (depth + 1) // 2
        nm_ = depth // 2
        self.wqkv_in = self.din("w_qkv", [max(na, 1), D, 3072])
        self.wo_in = self.din("w_o", [max(na, 1), D, D])
        self.sink_in = self.din("sink", [max(na, 1), 128, 16, 128])
        self.win_in = self.din("w_in", [max(nm_, 1), D, 6176])
        self.mwo_in = self.din("m_w_o", [max(nm_, 1), D, D])
        self.mbrow_in = self.din("m_brow", [max(nm_, 1), 128, 5152])
        self.mbqk_in = self.din("m_bqk", [max(nm_, 1), 128, 16])
        self.ghead_in = self.din("g_head", [max(nm_, 1), 128, D])
        self.tri_in = self.din("tri", [128, 2, 128])
        self.cos_in = self.din("cosT", [128, T])
        self.sin_in = self.din("sinT", [128, T])
        self.rperm_in = self.din("rperm", [128, 128])
        self.mask_in = self.din("mask", [128, 2, 512])
        self.out = nc.dram_tensor("out", [L, D], F32, kind="ExternalOutput")
        self.xt_dbg = nc.dram_tensor("xt_dbg", [D, T], F32, kind="ExternalOutput") if getattr(self, "debug", False) else None
        self.XT = self.dscr("XT", [D, T], F32)
        self.wup_b = [self.dscr("wup_b%d" % l, [NJ * 128, KC * 256], BF16) for l in range(depth)]
        self.wdn_b = [self.dscr("wdn_b%d" % l, [16 * 128, NJ * 128], BF16) for l in range(depth)]
        self.wqkv_b = [self.dscr("wqkv_b%d" % i, [D, 3072], BF16) for i in range(na)]
        self.wo_b = [self.dscr("wo_b%d" % i, [D, D], BF16) for i in range(na)]
        self.QT = self.dscr("QT", [128, 16, T], BF16)
        self.KT = self.dscr("KT", [128, 4, T], BF16)
        self.V = self.dscr("V", [T, 512], BF16)
        self.win_b = [self.dscr("win_b%d" % i, [D, 6176], BF16) for i in range(nm_)]
        self.mwo_b = [self.dscr("mwo_b%d" % i, [D, D], BF16) for i in range(nm_)]
        self.MQT = self.dscr("MQT", [128, 8, T], BF16)
        self.MKT = self.dscr("MKT", [128, 8, T], BF16)
        self.MK = self.dscr("MK", [T, 1024], BF16)
        self.MV = self.dscr("MV", [T, D], BF16)
        self.SIGO = self.dscr("SIGO", [T, D], BF16)
        self.GATES = self.dscr("GATES", [T, 32], F32)
        self.HF = self.dscr("HF", [T, D], F32)
        self.HB = self.dscr("HB", [T, D], F32)
        self.ps = [es.enter_context(nc.psum_tensor("ps%d" % i, [128, 512], F32)) for i in range(8)]
        self.ident = self.sb(es, "ident", [128, 128], F32)
        self.ones_b = self.sb(es, "ones_b", [128, 128], BF16)
        self.MOD = self.sb(es, "MOD", [128, depth, 96, 2], F32)
        self.A1 = self.sb(es, "A1", [128, depth, KC, 2], F32)
        self.A2 = self.sb(es, "A2", [128, depth, KC, 2], F32)
        self.gfin = self.sb(es, "gfin", [128, KC], F32)

        self.dma("sp", self.ident[:], self.ident_in.ap(), writes=["ident"])
        self.dma("sp", self.gfin[:], self.gfin_in.ap(), writes=["gfin"])
        s.op("pool", lambda h: h.memset(self.ones_b[:], 1.0), writes=["ones_b"])
        self.prologue()
        for si_, step in enumerate(self.plan):
            kind, l = step
            hook = (lambda si_=si_: self.cast_step(si_ + 1))
            if kind == "ffn":
                self.ffn(l, mid_hook=hook)
            elif kind == "att":
                self.att(l, start_hook=hook)
            elif kind == "mls":
                self.mls(l, mid_hook=hook)
        self.final()
        s.emit(es)
        es.close()
        return nc

    def XTv(self):
        return self.XT.ap().rearrange("(k p) t -> p k t", p=128)

    def cast_step(self, i):
        if i >= len(self.plan):
            return
        kind, l = self.plan[i]
        if kind == "mls":
            self.cast_weights(("win", l // 2), self.win_b[l // 2].ap(), self.win_in.ap()[l // 2], D, 6176)
            self.cast_weights(("mwo", l // 2), self.mwo_b[l // 2].ap(), self.mwo_in.ap()[l // 2], D, D)
        if kind == "att":
            self.cast_weights(("wqkv", l // 2), self.wqkv_b[l // 2].ap(), self.wqkv_in.ap()[l // 2], D, 3072)
            self.cast_weights(("wo", l // 2), self.wo_b[l // 2].ap(), self.wo_in.ap()[l // 2], D, D)
        if kind == "ffn":
            self.cast_weights(("wup", l), self.wup_b[l].ap(), self.wup_in.ap()[l], NJ * 128, KC * 256)
            self.cast_weights(("wdn", l), self.wdn_b[l].ap(), self.wdn_in.ap()[l], 16 * 128, NJ * 128)

    def prologue(self):
        nc, s = self.nc, self.s
        L, T, depth = self.L, self.T, self.depth
        self.cast_step(0)
        with ExitStack() as st:
            xin = [self.sb(st, "p_xin%d" % i, [128, D], F32) for i in range(2)]
            xo = [self.sb(st, "p_xo%d" % i, [128, KC, 128], F32) for i in range(2)]
            blocks = [(self.ctx_in, i, i * 128) for i in range(CT // 128)] + \
                     [(self.x_in, i, CT + i * 128) for i in range(L // 128)]
            XTv = self.XTv()
            for bi, (src, i, t0) in enumerate(blocks):
                a = bi % 2
                self.dma("sp", xin[a][:], src.ap()[i * 128:(i + 1) * 128, :], writes=["xin%d" % a])
                for q in range(4):
                    pb = (bi * 4 + q) % 8
                    for r in range(4):
                        kc = q * 4 + r
                        s.op("pe", lambda h, a=a, kc=kc, pb=pb, r=r: h.transpose(
                            self.ps[pb][:, r * 128:(r + 1) * 128], xin[a][:, kc * 128:(kc + 1) * 128], self.ident[:]),
                            reads=["xin%d" % a, "ident"], writes=["ps%d" % pb])
                    eng = "act" if q % 2 == 0 else "dve"
                    if eng == "act":
                        s.op("act", lambda h, a=a, q=q, pb=pb: h.activation(
                            out=xo[a][:, q * 4:(q + 1) * 4, :], in_=self.ps[pb][:, :], func=AF.Copy),
                            reads=["ps%d" % pb], writes=[("xo%d" % a, q, q + 1)])
                    else:
                        s.op("dve", lambda h, a=a, q=q, pb=pb: h.tensor_copy(
                            out=xo[a][:, q * 4:(q + 1) * 4, :], in_=self.ps[pb][:, :]),
                            reads=["ps%d" % pb], writes=[("xo%d" % a, q, q + 1)])
                self.dma("sp", XTv[:, :, t0:t0 + 128], xo[a][:], reads=["xo%d" % a], writes=[("XT", t0, t0 + 128)])
        s.barrier()
        with ExitStack() as st:
            cc = self.sb(st, "p_cc", [128, KC, 2], F32)
            scc = self.sb(st, "p_scc", [128, KC, 2], F32)
            bm = self.sb(st, "p_bm", [128, depth, 96, 2], F32)
            gm = self.sb(st, "p_gm", [128, depth, KC, 2], F32)
            gf = self.sb(st, "p_gf", [128, depth, KC, 2], F32)
            wm = [self.sb(st, "p_wm%d" % i, [128, KC, 512], F32) for i in range(3)]
            modrow = self.sb(st, "p_modrow", [2, 6 * D], F32)
            self.dma("sp", cc[:], self.cc_in.ap(), writes=["cc"])
            self.dma("sp", bm[:], self.bmod_in.ap().rearrange("l p c s -> p l c s"), writes=["bm"])
            self.dma("sp", gm[:], self.gmix_in.ap().rearrange("l p c s -> p l c s"), writes=["gm"])
            self.dma("sp", gf[:], self.gffn_in.ap().rearrange("l p c s -> p l c s"), writes=["gf"])
            s.op("act", lambda h: h.activation(out=scc[:], in_=cc[:], func=AF.Silu), reads=["cc"], writes=["scc"])
            layers = sorted(set(l for (_, l) in self.plan))
            n = 0
            for l in layers:
                wv = self.wmod_in.ap()[l].rearrange("(t p) f -> t p f", p=128)
                for ct in range(24):
                    a = n % 3
                    pb = 1 + n % 2
                    n += 1
                    self.dma("sp", wm[a][:], wv[ct].rearrange("p (k f) -> p k f", k=KC), writes=["wm%d" % a])
                    for kc in range(KC):
                        s.op("pe", lambda h, a=a, kc=kc, pb=pb: h.matmul(
                            self.ps[pb][0:2, 0:512], lhsT=scc[:, kc, :], rhs=wm[a][:, kc, :],
                            start=(kc == 0), stop=(kc == KC - 1)),
                            reads=["wm%d" % a, "scc"], writes=["ps%d" % pb])
                    if ct % 2 == 0:
                        s.op("act", lambda h, ct=ct, pb=pb: h.activation(
                            out=modrow[:, ct * 512:(ct + 1) * 512], in_=self.ps[pb][0:2, 0:512], func=AF.Copy),
                            reads=["ps%d" % pb], writes=[("modrow", ct, ct + 1)])
                    else:
                        s.op("dve", lambda h, ct=ct, pb=pb: h.tensor_copy(
                            out=modrow[:, ct * 512:(ct + 1) * 512], in_=self.ps[pb][0:2, 0:512]),
                            reads=["ps%d" % pb], writes=[("modrow", ct, ct + 1)])
                for c in range(96):
                    s.op("pe", lambda h, c=c: h.transpose(self.ps[0][:, c * 2:c * 2 + 2],
                                                          modrow[:, c * 128:(c + 1) * 128], self.ident[0:2, 0:2]),
                         reads=[("modrow", c // 4, c // 4 + 1), "ident"], writes=["ps0"])
                s.op("dve", lambda h, l=l: h.tensor_tensor(
                    out=self.MOD[:, l, :, :], in0=self.ps[0][:, 0:192].rearrange("p (c s) -> p c s", s=2),
                    in1=bm[:, l, :, :], op=ALU.add),
                    reads=["ps0", "bm"], writes=["MOD"])
                s.op("dve", lambda h, l=l: h.scalar_tensor_tensor(
                    out=self.A1[:, l, :, :], in0=self.MOD[:, l, 16:32, :], scalar=1.0, in1=gm[:, l, :, :],
                    op0=ALU.add, op1=ALU.mult), reads=["MOD", "gm"], writes=["A1"])
                s.op("dve", lambda h, l=l: h.scalar_tensor_tensor(
                    out=self.A2[:, l, :, :], in0=self.MOD[:, l, 64:80, :], scalar=1.0, in1=gf[:, l, :, :],
                    op0=ALU.add, op1=ALU.mult), reads=["MOD", "gf"], writes=["A2"])
            s.barrier()

    def norm_mod(self, xt, xname, sq, h, hname, n, A, B, scr, scrname, psb=6):
        s = self.s
        half = KC // 2
        s.op("act", lambda e: e.activation(out=sq[:, 0:half, 0:n], in_=xt[:, 0:half, 0:n], func=AF.Square),
             reads=[xname], writes=[("sq", 0, half)])
        s.op("pool", lambda e: e.tensor_tensor(out=sq[:, half:KC, 0:n], in0=xt[:, half:KC, 0:n],
                                                in1=xt[:, half:KC, 0:n], op=ALU.mult),
             reads=[xname], writes=[("sq", half, KC)])
        pname = "ps%d" % psb
        for kc in range(KC):
            s.op("pe", lambda e, kc=kc: e.matmul(self.ps[psb][:, 0:n], lhsT=self.ones_b[:], rhs=sq[:, kc, 0:n],
                                                 start=(kc == 0), stop=(kc == KC - 1)),
                 reads=[("sq", kc, kc + 1), "ones_b"], writes=[pname])
        s.op("act", lambda e: e.activation(out=scr[:, 0:n], in_=self.ps[psb][:, 0:n], func=AF.Sqrt,
                                           scale=1.0 / D, bias=EPS),
             reads=[pname], writes=[scrname])
        s.op("dve", lambda e: e.reciprocal(out=scr[:, 0:n], in_=scr[:, 0:n]), reads=[scrname], writes=[scrname])
        for kc in range(KC):
            me = "dve" if kc % 2 == 0 else "pool"
            s.op(me, lambda e, kc=kc: e.tensor_tensor(out=xt[:, kc, 0:n], in0=xt[:, kc, 0:n], in1=scr[:, 0:n],
                                                      op=ALU.mult),
                 reads=[(xname, kc, kc + 1), scrname], writes=[(xname, kc, kc + 1)])
            if kc % 2 == 0:
                s.op("act", lambda e, kc=kc: e.activation(out=h[:, kc, 0:n], in_=xt[:, kc, 0:n], func=AF.Identity,
                                                          scale=A(kc), bias=B(kc)),
                     reads=[(xname, kc, kc + 1)], writes=[(hname, kc, kc + 1)])
            else:
                s.op("dve", lambda e, kc=kc: e.tensor_scalar(out=h[:, kc, 0:n], in0=xt[:, kc, 0:n], scalar1=A(kc),
                                                             scalar2=B(kc), op0=ALU.mult, op1=ALU.add),
                     reads=[(xname, kc, kc + 1)], writes=[(hname, kc, kc + 1)])

    def ffn(self, l, mid_hook=None):
        nc, s = self.nc, self.s
        L, T = self.L, self.T
        NW = 464
        tiles = seq_tiles(CT, 1, 0)
        for t in tiles:
            t["s"] = 1
        nlt = max(1, -(-L // 455))
        lt = seq_tiles(L, nlt, CT)
        for t in lt:
            t["s"] = 0
        tiles = (tiles if l != self.depth - 1 or not self.plan_is_full() else []) + lt
        XTv = self.XTv()
        wupv = self.wup_b[l].ap().rearrange("(j p) f -> j p f", p=128)
        wdnv = self.wdn_b[l].ap().rearrange("(c p) f -> c p f", p=128)
        with ExitStack() as st:
            xt = self.sb(st, "f_xt", [128, KC, NW], F32)
            sq = self.sb(st, "f_sq", [128, KC, NW], BF16)
            hb = self.sb(st, "f_h", [128, KC, NW], BF16)
            g = self.sb(st, "f_g", [128, NJ, NW], BF16)
            E = [self.sb(st, "f_E%d" % i, [128, 516], F32) for i in range(4)]
            acc = [self.sb(st, "f_acc%d" % i, [128, NW], F32) for i in range(4)]
            sa = [self.sb(st, "f_sa%d" % i, [128, NW], F32) for i in range(2)]
            xr = [self.sb(st, "f_xr%d" % i, [128, NW], F32) for i in range(3)]
            xo = [self.sb(st, "f_xo%d" % i, [128, NW], F32) for i in range(3)]
            scr = self.sb(st, "f_scr", [128, NW], F32)
            wup = [self.sb(st, "f_wup%d" % i, [128, KC, 256], BF16) for i in range(4)]
            wdn = [self.sb(st, "f_wdn%d" % i, [128, NJ, 128], BF16) for i in range(3)]
            H = self.sb(st, "f_H", [128, 88, 2], F32)
            cw = self.sb(st, "f_cw", [128, 88, 3], F32)
            cb = self.sb(st, "f_cb", [128, 88], F32)
            self.dma("sp", cw[:], self.cw_in.ap()[l], writes=["cw"])
            self.dma("sp", cb[:], self.cb_in.ap()[l], writes=["cb"])

            units = []
            for ti in range(len(tiles)):
                units += [("up", j) for j in range(NJ)] + [("dn", c) for c in range(16)]
            st_ = dict(nload=0, nup=0, ndn=0)
            slot_of = {}

            def ensure_loaded(upto):
                while st_["nload"] <= min(upto, len(units) - 1):
                    u = st_["nload"]
                    kind, idx = units[u]
                    if kind == "up":
                        a = st_["nup"] % 4
                        st_["nup"] += 1
                        self.dma("sp", wup[a][:], wupv[idx].rearrange("p (k f) -> p k f", k=KC),
                                 writes=["wup%d" % a], extra=self.bg[("wup", l)])
                    else:
                        a = st_["ndn"] % 3
                        st_["ndn"] += 1
                        self.dma("sp", wdn[a][:], wdnv[idx].rearrange("p (j f) -> p j f", j=NJ),
                                 writes=["wdn%d" % a], extra=self.bg[("wdn", l)])
                    slot_of[u] = a
                    st_["nload"] += 1

            def load_x(t):
                n = t["m_hi"] - t["m_lo"]
                self.dma("sp", xt[:, :, 0:n], XTv[:, :, t["m_lo"]:t["m_hi"]],
                         reads=[("XT", t["m_lo"], t["m_hi"])], writes=["f_xt"])

            def do_norm(t):
                n = t["m_hi"] - t["m_lo"]
                sidx = t["s"]
                self.norm_mod(xt, "f_xt", sq, hb, "f_h", n,
                              lambda kc: self.A2[:, l, kc, sidx:sidx + 1],
                              lambda kc: self.MOD[:, l, 48 + kc, sidx:sidx + 1], scr, "f_scr")

            bank = [0]
            ecnt = [0]
            ucur = [0]
            load_x(tiles[0])
            do_norm(tiles[0])
            def half_body(t, j, half, a, nm, no, p_off):
                f = j + NJ * half
                pb = bank[0] % 6
                bank[0] += 1
                for kc in range(KC):
                    s.op("pe", lambda e, kc=kc: e.matmul(
                        self.ps[pb][:, 0:nm], lhsT=wup[a][:, kc, half * 128:(half + 1) * 128],
                        rhs=hb[:, kc, 0:nm], start=(kc == 0), stop=(kc == KC - 1)),
                        reads=["wup%d" % a, ("f_h", kc, kc + 1)], writes=["ps%d" % pb])
                ei = ecnt[0] % 4
                ecnt[0] += 1
                Eb, ab = E[ei], acc[ei]
                en, an = "f_E%d" % ei, "f_acc%d" % ei
                if t["first"]:
                    s.op("pool", lambda e: e.memset(Eb[:, 0:1], 0.0), writes=[(en, 0, 1)])
                else:
                    s.op("pool", lambda e: e.tensor_copy(out=Eb[:, 0:2], in_=H[:, f, :]),
                         reads=[("H", f, f + 1)], writes=[(en, 0, 2)])
                s.op("act", lambda e: e.activation(
                    out=Eb[:, p_off:p_off + nm], in_=self.ps[pb][:, 0:nm], func=AF.Copy),
                    reads=["ps%d" % pb], writes=[(en, p_off, p_off + nm)])
                if t["last"]:
                    s.op("pool", lambda e: e.memset(Eb[:, no + 1:no + 2], 0.0),
                         writes=[(en, no + 1, no + 2)])
                else:
                    s.op("pool", lambda e: e.tensor_copy(
                        out=H[:, f, :], in_=Eb[:, p_off + nm - 2:p_off + nm]),
                        reads=[(en, p_off + nm - 2, p_off + nm)], writes=[("H", f, f + 1)])
                s.op("pool", lambda e: e.tensor_scalar(
                    out=ab[:, 0:no], in0=Eb[:, 0:no], scalar1=cw[:, f, 0:1], scalar2=cb[:, f:f + 1],
                    op0=ALU.mult, op1=ALU.add),
                    reads=[en, "cw", "cb"], writes=[an])
                s.op("dve", lambda e: e.scalar_tensor_tensor(
                    out=ab[:, 0:no], in0=Eb[:, 1:no + 1], scalar=cw[:, f, 1:2], in1=ab[:, 0:no],
                    op0=ALU.mult, op1=ALU.add), reads=[en, an, "cw"], writes=[an])
                s.op("dve", lambda e: e.scalar_tensor_tensor(
                    out=ab[:, 0:no], in0=Eb[:, 2:no + 2], scalar=cw[:, f, 2:3], in1=ab[:, 0:no],
                    op0=ALU.mult, op1=ALU.add), reads=[en, an, "cw"], writes=[an])
                return (ab, an)

            def pair_body(t, j, nm, no, p_off):
                u = ucur[0]
                ucur[0] += 1
                ensure_loaded(u + 2)
                a = slot_of[u]
                accs = [half_body(t, j, half, a, nm, no, p_off) for half in range(2)]
                si = j % 2
                s.op("act", lambda e: e.activation(out=sa[si][:, 0:no], in_=accs[0][0][:, 0:no], func=AF.Silu),
                     reads=[accs[0][1]], writes=["f_sa%d" % si])
                s.op("pool", lambda e: e.tensor_tensor(
                    out=g[:, j, 0:no], in0=sa[si][:, 0:no], in1=accs[1][0][:, 0:no], op=ALU.mult),
                    reads=["f_sa%d" % si, accs[1][1]], writes=[("f_g", j, j + 1)])

            def dn_body(t, c, no, sidx):
                u = ucur[0]
                ucur[0] += 1
                ensure_loaded(u + 2)
                a = slot_of[u]
                ri = c % 3
                self.dma("sp", xr[ri][:, 0:no], XTv[:, c, t["o_lo"]:t["o_hi"]],
                         reads=[("XT", t["o_lo"], t["o_hi"])], writes=["f_xr%d" % ri])
                pb = bank[0] % 6
                bank[0] += 1
                for j in range(NJ):
                    s.op("pe", lambda e, j=j: e.matmul(
                        self.ps[pb][:, 0:no], lhsT=wdn[a][:, j, :], rhs=g[:, j, 0:no],
                        start=(j == 0), stop=(j == NJ - 1)),
                        reads=["wdn%d" % a, ("f_g", j, j + 1)], writes=["ps%d" % pb])
                s.op("dve", lambda e: e.scalar_tensor_tensor(
                    out=xo[ri][:, 0:no], in0=self.ps[pb][:, 0:no], scalar=self.MOD[:, l, 80 + c, sidx:sidx + 1],
                    in1=xr[ri][:, 0:no], op0=ALU.mult, op1=ALU.add),
                    reads=["ps%d" % pb, "f_xr%d" % ri], writes=["f_xo%d" % ri])
                self.dma("sp", XTv[:, c, t["o_lo"]:t["o_hi"]], xo[ri][:, 0:no],
                         reads=["f_xo%d" % ri], writes=[("XT", t["o_lo"], t["o_hi"])])

            for ti, t in enumerate(tiles):
                nm = t["m_hi"] - t["m_lo"]
                no = t["o_hi"] - t["o_lo"]
                p_off = 1 if t["first"] else 2
                for j in range(NJ):
                    pair_body(t, j, nm, no, p_off)
                if mid_hook is not None and ti == min(2, len(tiles) - 1):
                    mid_hook()
                if ti + 1 < len(tiles):
                    load_x(tiles[ti + 1])
                    do_norm(tiles[ti + 1])
                for c in range(16):
                    dn_body(t, c, no, t["s"])
        s.barrier()

    def att(self, l, mid_hook=None, start_hook=None):
        nc, s = self.nc, self.s
        L, T = self.L, self.T
        j = l // 2
        NB = T // 128
        XTv = self.XTv()
        wq_v = self.wqkv_b[j].ap().rearrange("(k p) f -> p k f", p=128)
        with ExitStack() as st:
            wq = self.sb(st, "a_wq", [128, KC, 3072], BF16)
            xt = self.sb(st, "a_xt", [128, KC, 512], F32)
            sq = self.sb(st, "a_sq", [128, KC, 512], BF16)
            hb = self.sb(st, "a_h", [128, KC, 512], BF16)
            scr = self.sb(st, "a_scr", [128, 512], F32)
            cs = self.sb(st, "a_cs", [128, 2, 512], F32)
            rp = self.sb(st, "a_rp", [128, 128], BF16)
            rpf = self.sb(st, "a_rpf", [128, 128], F32)
            qsb = [self.sb(st, "a_qsb%d" % i, [128, 512], BF16) for i in range(2)]
            t1 = [self.sb(st, "a_t1%d" % i, [128, 512], F32) for i in range(2)]
            t2 = [self.sb(st, "a_t2%d" % i, [128, 512], F32) for i in range(2)]
            qo = [self.sb(st, "a_qo%d" % i, [128, 512], BF16) for i in range(3)]
            vo = [self.sb(st, "a_vo%d" % i, [128, 512], BF16) for i in range(2)]
            for kc in range(KC):
                self.dma("sp", wq[:, kc, :], wq_v[:, kc, :], writes=[("a_wq", kc, kc + 1)], extra=self.bg[("wqkv", j)])
            self.dma("sp", rpf[:], self.rperm_in.ap(), writes=["a_rpf"])
            if start_hook is not None:
                start_hook()
            s.op("act", lambda e: e.activation(out=rp[:], in_=rpf[:], func=AF.Copy), reads=["a_rpf"], writes=["a_rp"])
            tiles = [(0, CT, 1)] + [(CT + i * 512, min(T, CT + (i + 1) * 512), 0) for i in range(-(-L // 512))]
            cnt = [0]

            def qk_mm(t0, n, hc):
                i = cnt[0]
                cnt[0] += 1
                pa, pb = (i % 3) * 2, (i % 3) * 2 + 1
                a2 = i % 2
                a3 = i % 3
                for kc in range(KC):
                    s.op("pe", lambda e, kc=kc: e.matmul(self.ps[pa][:, 0:n], lhsT=wq[:, kc, hc * 128:(hc + 1) * 128],
                                                         rhs=hb[:, kc, 0:n], start=(kc == 0), stop=(kc == KC - 1)),
                         reads=[("a_wq", kc, kc + 1), ("a_h", kc, kc + 1)], writes=["ps%d" % pa])
                s.op("act", lambda e: e.activation(out=qsb[a2][:, 0:n], in_=self.ps[pa][:, 0:n], func=AF.Copy),
                     reads=["ps%d" % pa], writes=["a_qsb%d" % a2])
                return i

            def qk_rope(t0, n, hc, i):
                pa, pb = (i % 3) * 2, (i % 3) * 2 + 1
                a2 = i % 2
                a3 = i % 3
                s.op("pe", lambda e: e.matmul(self.ps[pb][:, 0:n], lhsT=rp[:], rhs=qsb[a2][:, 0:n], start=True, stop=True),
                     reads=["a_rp", "a_qsb%d" % a2], writes=["ps%d" % pb])
                s.op("dve", lambda e: e.scalar_tensor_tensor(out=t1[a2][:, 0:n], in0=self.ps[pa][:, 0:n], scalar=1.0,
                                                             in1=cs[:, 0, 0:n], op0=ALU.mult, op1=ALU.mult),
                     reads=["ps%d" % pa, "a_cs", "a_qsb%d" % a2], writes=["a_t1%d" % a2])
                s.op("dve", lambda e: e.scalar_tensor_tensor(out=t2[a2][:, 0:n], in0=self.ps[pb][:, 0:n], scalar=1.0,
                                                             in1=cs[:, 1, 0:n], op0=ALU.mult, op1=ALU.mult),
                     reads=["ps%d" % pb, "a_cs"], writes=["a_t2%d" % a2])
                s.op("pool", lambda e: e.tensor_tensor(out=qo[a3][:, 0:n], in0=t1[a2][:, 0:n], in1=t2[a2][:, 0:n],
                                                       op=ALU.add),
                     reads=["a_t1%d" % a2, "a_t2%d" % a2], writes=["a_qo%d" % a3])
                if hc < 16:
                    dst = self.QT.ap()[:, hc, t0:t0 + n]
                    wr = ("QT", t0, t0 + n)
                else:
                    dst = self.KT.ap()[:, hc - 16, t0:t0 + n]
                    wr = ("KT", t0, t0 + n)
                self.dma("sp", dst, qo[a3][:, 0:n], reads=["a_qo%d" % a3], writes=[wr])

            def v_block(t0, blk):
                i = cnt[0]
                cnt[0] += 1
                pa = (i % 3) * 2
                a2 = i % 2
                for kc in range(KC):
                    s.op("pe", lambda e, kc=kc: e.matmul(self.ps[pa][:, 0:512], lhsT=hb[:, kc, blk * 128:(blk + 1) * 128],
                                                         rhs=wq[:, kc, 2560:3072], start=(kc == 0), stop=(kc == KC - 1)),
                         reads=[("a_wq", kc, kc + 1), ("a_h", kc, kc + 1)], writes=["ps%d" % pa])
                s.op("act", lambda e: e.activation(out=vo[a2][:], in_=self.ps[pa][:, 0:512], func=AF.Copy),
                     reads=["ps%d" % pa], writes=["a_vo%d" % a2])
                tb = t0 + blk * 128
                self.dma("sp", self.V.ap()[tb:tb + 128, :], vo[a2][:], reads=["a_vo%d" % a2], writes=[("V", tb, tb + 128)])

            def a1_tile(t0, t1_, sidx):
                n = t1_ - t0
                self.dma("sp", xt[:, :, 0:n], XTv[:, :, t0:t1_], reads=[("XT", t0, t1_)], writes=["a_xt"])
                self.dma("sp", cs[:, 0, 0:n], self.cos_in.ap()[:, t0:t1_], writes=[("a_cs", 0, 1)])
                self.dma("sp", cs[:, 1, 0:n], self.sin_in.ap()[:, t0:t1_], writes=[("a_cs", 1, 2)])
                self.norm_mod(xt, "a_xt", sq, hb, "a_h", n,
                              lambda kc: self.A1[:, l, kc, sidx:sidx + 1],
                              lambda kc: self.MOD[:, l, kc, sidx:sidx + 1], scr, "a_scr", psb=6)
                prev = None
                for hc in range(20):
                    i = qk_mm(t0, n, hc)
                    if prev is not None:
                        qk_rope(t0, n, prev[0], prev[1])
                    prev = (hc, i)
                v_block(t0, 0)
                qk_rope(t0, n, prev[0], prev[1])
                for blk in range(1, n // 128):
                    v_block(t0, blk)

            for (t0, t1_, sidx) in tiles:
                a1_tile(t0, t1_, sidx)
        s.barrier()
        if mid_hook is not None:
            mid_hook()
        with ExitStack() as st:
            kt = self.sb(st, "b_kt", [128, 4, T], BF16)
            vv = self.sb(st, "b_v", [128, NB, 512], BF16)
            wo = self.sb(st, "b_wo", [128, 16, D], BF16)
            mk = self.sb(st, "b_mk", [128, 2, 512], F32)
            esr = self.sb(st, "b_esr", [128, 16, 128], F32)
            qt = [self.sb(st, "b_qt%d" % i, [128, 16, 128], BF16) for i in range(2)]
            P = [self.sb(st, "b_P%d" % i, [128, 512], BF16) for i in range(10)]
            tm = [self.sb(st, "b_tm%d" % i, [128, 512], F32) for i in range(2)]
            rd = [self.sb(st, "b_rd%d" % i, [128, 512], F32) for i in range(2)]
            ot = [self.sb(st, "b_ot%d" % i, [128, 16, 128], BF16) for i in range(2)]
            xr1 = self.sb(st, "b_xr0", [128, KC, 128], F32)
            xo1 = self.sb(st, "b_xo0", [128, KC, 128], F32)
            xr = [xr1, xr1]
            xo = [xo1, xo1]
            for g in range(4):
                self.dma("sp", kt[:, g, :], self.KT.ap()[:, g, :], reads=["KT"], writes=["b_kt"])
            self.dma("sp", vv[:], self.V.ap().rearrange("(b p) f -> p b f", p=128), reads=["V"], writes=["b_v"])
            wo_v = self.wo_b[j].ap().rearrange("(h p) f -> p h f", p=128)
            for hd in range(16):
                self.dma("sp", wo[:, hd, :], wo_v[:, hd, :], writes=["b_wo"], extra=self.bg[("wo", j)])
            self.dma("sp", mk[:], self.mask_in.ap(), writes=["b_mk"])
            self.dma("sp", esr[:], self.sink_in.ap()[j], writes=["b_esr"])
            s.op("act", lambda e: e.activation(out=esr[:], in_=esr[:], func=AF.Exp), reads=["b_esr"], writes=["b_esr"])
            nlat = L // 128
            sc_ = 128 ** -0.5
            pcnt = [0]
            scnt = [0]
            mcnt = [0]

            def group(bq, gq, kbs, a):
                Ps = []
                for (kb, mi) in kbs:
                    pi = pcnt[0] % 10
                    pcnt[0] += 1
                    sb_ = scnt[0] % 3
                    scnt[0] += 1
                    s.op("pe", lambda e, kb=kb, sb_=sb_: e.matmul(
                        self.ps[sb_][:, :], lhsT=kt[:, gq, kb * 128:(kb + 1) * 128],
                        rhs=qt[a][:, gq * 4:(gq + 1) * 4, :].rearrange("p h q -> p (h q)"), start=True, stop=True),
                        reads=["b_kt", "b_qt%d" % a], writes=["ps%d" % sb_])
                    if mi is None:
                        s.op("act", lambda e, pi=pi, sb_=sb_: e.activation(out=P[pi][:], in_=self.ps[sb_][:, :],
                                                                           func=AF.Exp, scale=sc_),
                             reads=["ps%d" % sb_], writes=["b_P%d" % pi])
                    else:
                        ti = mcnt[0] % 2
                        mcnt[0] += 1
                        s.op("dve", lambda e, ti=ti, sb_=sb_, mi=mi: e.scalar_tensor_tensor(
                            out=tm[ti][:], in0=self.ps[sb_][:, :], scalar=1.0, in1=mk[:, mi, :], op0=ALU.mult,
                            op1=ALU.add),
                            reads=["ps%d" % sb_, "b_mk"], writes=["b_tm%d" % ti])
                        s.op("act", lambda e, pi=pi, ti=ti: e.activation(out=P[pi][:], in_=tm[ti][:], func=AF.Exp,
                                                                         scale=sc_),
                             reads=["b_tm%d" % ti], writes=["b_P%d" % pi])
                    Ps.append((kb, pi))
                nk = len(Ps)
                for i, (kb, pi) in enumerate(Ps):
                    s.op("pe", lambda e, pi=pi, i=i: e.matmul(self.ps[3][:, :], lhsT=self.ones_b[:], rhs=P[pi][:],
                                                              start=(i == 0), stop=(i == nk - 1)),
                         reads=["b_P%d" % pi, "ones_b"], writes=["ps3"])
                for hh in range(4):
                    for i, (kb, pi) in enumerate(Ps):
                        s.op("pe", lambda e, pi=pi, i=i, kb=kb, hh=hh: e.matmul(
                            self.ps[4][:, hh * 128:(hh + 1) * 128], lhsT=vv[:, kb, gq * 128:(gq + 1) * 128],
                            rhs=P[pi][:, hh * 128:(hh + 1) * 128], start=(i == 0), stop=(i == nk - 1)),
                            reads=["b_P%d" % pi, "b_v"], writes=["ps4"])
                ri = gq % 2
                s.op("dve", lambda e: e.scalar_tensor_tensor(
                    out=rd[ri][:], in0=self.ps[3][:, :], scalar=1.0,
                    in1=esr[:, gq * 4:(gq + 1) * 4, :].rearrange("p h q -> p (h q)"), op0=ALU.mult, op1=ALU.add),
                    reads=["ps3", "b_esr"], writes=["b_rd%d" % ri])
                s.op("act", lambda e: e.activation(out=rd[ri][:], in_=rd[ri][:], func=AF.Ln), reads=["b_rd%d" % ri],
                     writes=["b_rd%d" % ri])
                s.op("act", lambda e: e.activation(out=rd[ri][:], in_=rd[ri][:], func=AF.Exp, scale=-1.0),
                     reads=["b_rd%d" % ri], writes=["b_rd%d" % ri])
                s.op("dve", lambda e: e.scalar_tensor_tensor(
                    out=ot[a][:, gq * 4:(gq + 1) * 4, :].rearrange("p h q -> p (h q)"), in0=self.ps[4][:, :],
                    scalar=1.0, in1=rd[ri][:], op0=ALU.mult, op1=ALU.mult),
                    reads=["ps4", "b_rd%d" % ri], writes=[("b_ot%d" % a, gq, gq + 1)])

            def qblock(bq):
                a = bq % 2
                t0 = bq * 128
                sidx = 1 if bq < 2 else 0
                self.dma("sp", qt[a][:], self.QT.ap()[:, :, t0:t0 + 128], reads=[("QT", t0, t0 + 128)],
                         writes=["b_qt%d" % a])
                kbs = [(0, None), (1, None)]
                if bq >= 2:
                    n = bq - 2
                    if n > 0:
                        kbs.append((bq - 1, 0))
                    kbs.append((bq, None))
                    if n < nlat - 1:
                        kbs.append((bq + 1, 1))
                for gq in range(4):
                    group(bq, gq, kbs, a)

            def qblock_back(bq):
                a = bq % 2
                t0 = bq * 128
                sidx = 1 if bq < 2 else 0
                self.dma("sp", xr[a][:], XTv[:, :, t0:t0 + 128], reads=[("XT", t0, t0 + 128)], writes=["b_xr0"])
                for q4 in range(4):
                    pb = 5 + (bq * 4 + q4) % 3
                    for r in range(4):
                        dc = q4 * 4 + r
                        for hd in range(16):
                            s.op("pe", lambda e, dc=dc, hd=hd, r=r, pb=pb: e.matmul(
                                self.ps[pb][:, r * 128:(r + 1) * 128], lhsT=wo[:, hd, dc * 128:(dc + 1) * 128],
                                rhs=ot[a][:, hd, :], start=(hd == 0), stop=(hd == 15)),
                                reads=["b_wo", "b_ot%d" % a], writes=["ps%d" % pb])
                    for r in range(4):
                        dc = q4 * 4 + r
                        s.op("dve", lambda e, dc=dc, r=r, pb=pb: e.scalar_tensor_tensor(
                            out=xo[a][:, dc, :], in0=self.ps[pb][:, r * 128:(r + 1) * 128],
                            scalar=self.MOD[:, l, 32 + dc, sidx:sidx + 1], in1=xr[a][:, dc, :],
                            op0=ALU.mult, op1=ALU.add),
                            reads=["ps%d" % pb, "b_xr0"], writes=[("b_xo0", dc, dc + 1)])
                self.dma("sp", XTv[:, :, t0:t0 + 128], xo[a][:], reads=["b_xo0"], writes=[("XT", t0, t0 + 128)])

            for bq in range(NB):
                qblock(bq)
                if bq > 0:
                    qblock_back(bq - 1)
            qblock_back(NB - 1)
        s.barrier()

    def mls(self, l, mid_hook=None):
        nc, s = self.nc, self.s
        L, T = self.L, self.T
        j = l // 2
        NB = T // 128
        CHS = 128
        NCH = T // CHS
        NCTX = CT // CHS
        XTv = self.XTv()
        win_v = self.win_b[j].ap().rearrange("(k p) f -> p k f", p=128)
        KS = 128 ** -0.5
        with ExitStack() as st:
            xt = self.sb(st, "m_xt", [128, KC, 512], F32)
            sq = self.sb(st, "m_sq", [128, KC, 512], BF16)
            hb = self.sb(st, "m_h", [128, KC, 512], BF16)
            scr = self.sb(st, "m_scr", [128, 512], F32)
            wt = [self.sb(st, "m_wt%d" % i, [128, KC, 512], BF16) for i in range(3)]
            brow = self.sb(st, "m_brow", [128, 5152], F32)
            bqk = self.sb(st, "m_bqk", [128, 16], F32)
            oa = [self.sb(st, "m_oa%d" % i, [128, 512], BF16) for i in range(3)]
            ob = [self.sb(st, "m_ob%d" % i, [128, 512], BF16) for i in range(3)]
            of = [self.sb(st, "m_of%d" % i, [128, 512], F32) for i in range(2)]
            og = [self.sb(st, "m_og%d" % i, [128, 32], F32) for i in range(2)]
            self.dma("sp", brow[:], self.mbrow_in.ap()[j], writes=["m_brow"])
            self.dma("sp", bqk[:], self.mbqk_in.ap()[j], writes=["m_bqk"])
            s.op("dve", lambda e: e.tensor_scalar(out=bqk[:, 8:16], in0=bqk[:, 8:16], scalar1=KS, scalar2=None,
                                                  op0=ALU.mult), reads=["m_bqk"], writes=["m_bqk"])
            s.op("dve", lambda e: e.tensor_scalar(out=brow[:, 0:1024], in0=brow[:, 0:1024], scalar1=KS, scalar2=None,
                                                  op0=ALU.mult), reads=["m_brow"], writes=["m_brow"])
            tiles = [(0, CT, 1)] + [(CT + i * 512, min(T, CT + (i + 1) * 512), 0) for i in range(-(-L // 512))]
            units = [(0, 512, "q"), (512, 512, "q"), (1024, 512, "k"), (1536, 512, "k")] + \
                    [(2048 + i * 512, 512, "v") for i in range(4)] + [(4096 + i * 512, 512, "o") for i in range(4)] + \
                    [(6144, 32, "g")]
            cnt = dict(w=0, p=0, a=0, b=0, f=0, g=0)

            def load_unit(u):
                col0, ncol, kind = u
                a = cnt["w"] % 3
                cnt["w"] += 1
                self.dma("sp", wt[a][:, :, 0:ncol], win_v[:, :, col0:col0 + ncol], writes=["m_wt%d" % a],
                         extra=self.bg[("win", j)])
                return a

            def feat_major(t0, n, a, u, c):
                col0, ncol, kind = u
                ch = (col0 + c * 128) // 128
                pb = cnt["p"] % 6
                cnt["p"] += 1
                for kc in range(KC):
                    s.op("pe", lambda e, kc=kc: e.matmul(self.ps[pb][:, 0:n], lhsT=wt[a][:, kc, c * 128:(c + 1) * 128],
                                                         rhs=hb[:, kc, 0:n], start=(kc == 0), stop=(kc == KC - 1)),
                         reads=["m_wt%d" % a, ("m_h", kc, kc + 1)], writes=["ps%d" % pb])
                oi = cnt["a"] % 3
                cnt["a"] += 1
                sc = KS if kind == "k" else 1.0
                s.op("act", lambda e: e.activation(out=oa[oi][:, 0:n], in_=self.ps[pb][:, 0:n], func=AF.Identity,
                                                   scale=sc, bias=bqk[:, ch:ch + 1]),
                     reads=["ps%d" % pb, "m_bqk"], writes=["m_oa%d" % oi])
                dst = (self.MQT if kind == "q" else self.MKT).ap()[:, ch % 8, t0:t0 + n]
                self.dma("sp", dst, oa[oi][:, 0:n], reads=["m_oa%d" % oi],
                         writes=[("MQT" if kind == "q" else "MKT", t0, t0 + n)])

            def tok_major(t0, blk, a, u):
                col0, ncol, kind = u
                pb = cnt["p"] % 6
                cnt["p"] += 1
                tb = t0 + blk * 128
                for kc in range(KC):
                    s.op("pe", lambda e, kc=kc: e.matmul(self.ps[pb][:, 0:ncol], lhsT=hb[:, kc, blk * 128:(blk + 1) * 128],
                                                         rhs=wt[a][:, kc, 0:ncol], start=(kc == 0), stop=(kc == KC - 1)),
                         reads=["m_wt%d" % a, ("m_h", kc, kc + 1)], writes=["ps%d" % pb])
                bcol = col0 - 1024
                if kind in ("k", "v"):
                    oi = cnt["b"] % 3
                    cnt["b"] += 1
                    sc = KS if kind == "k" else 1.0
                    s.op("dve", lambda e: e.scalar_tensor_tensor(
                        out=ob[oi][:, :], in0=self.ps[pb][:, 0:512], scalar=sc, in1=brow[:, bcol:bcol + 512],
                        op0=ALU.mult, op1=ALU.add), reads=["ps%d" % pb, "m_brow"], writes=["m_ob%d" % oi])
                    if kind == "k":
                        dst = self.MK.ap()[tb:tb + 128, col0 - 1024:col0 - 1024 + 512]
                        wr = ("MK", tb, tb + 128)
                    else:
                        dst = self.MV.ap()[tb:tb + 128, col0 - 2048:col0 - 2048 + 512]
                        wr = ("MV", tb, tb + 128)
                    self.dma("sp", dst, ob[oi][:, :], reads=["m_ob%d" % oi], writes=[wr])
                elif kind == "o":
                    fi = cnt["f"] % 2
                    cnt["f"] += 1
                    oi = cnt["b"] % 3
                    cnt["b"] += 1
                    s.op("dve", lambda e: e.scalar_tensor_tensor(
                        out=of[fi][:, :], in0=self.ps[pb][:, 0:512], scalar=1.0, in1=brow[:, bcol:bcol + 512],
                        op0=ALU.mult, op1=ALU.add), reads=["ps%d" % pb, "m_brow"], writes=["m_of%d" % fi])
                    s.op("act", lambda e: e.activation(out=ob[oi][:, :], in_=of[fi][:, :], func=AF.Sigmoid),
                         reads=["m_of%d" % fi], writes=["m_ob%d" % oi])
                    self.dma("sp", self.SIGO.ap()[tb:tb + 128, col0 - 4096:col0 - 4096 + 512], ob[oi][:, :],
                             reads=["m_ob%d" % oi], writes=[("SIGO", tb, tb + 128)])
                else:
                    gi = cnt["g"] % 2
                    cnt["g"] += 1
                    s.op("dve", lambda e: e.scalar_tensor_tensor(
                        out=og[gi][:, :], in0=self.ps[pb][:, 0:32], scalar=1.0, in1=brow[:, bcol:bcol + 32],
                        op0=ALU.mult, op1=ALU.add), reads=["ps%d" % pb, "m_brow"], writes=["m_og%d" % gi])
                    self.dma("sp", self.GATES.ap()[tb:tb + 128, :], og[gi][:, :], reads=["m_og%d" % gi],
                             writes=[("GATES", tb, tb + 128)])

            def m1_tile(t0, t1_, sidx):
                n = t1_ - t0
                self.dma("sp", xt[:, :, 0:n], XTv[:, :, t0:t1_], reads=[("XT", t0, t1_)], writes=["m_xt"])
                self.norm_mod(xt, "m_xt", sq, hb, "m_h", n,
                              lambda kc: self.A1[:, l, kc, sidx:sidx + 1],
                              lambda kc: self.MOD[:, l, kc, sidx:sidx + 1], scr, "m_scr", psb=6)
                nxt = load_unit(units[0])
                for ui, u in enumerate(units):
                    a = nxt
                    if ui + 1 < len(units):
                        nxt = load_unit(units[ui + 1])
                    if u[2] in ("q", "k"):
                        for c in range(4):
                            feat_major(t0, n, a, u, c)
                    if u[2] != "q":
                        for blk in range(n // 128):
                            tok_major(t0, blk, a, u)

            for (t0, t1_, sidx) in tiles:
                m1_tile(t0, t1_, sidx)
        s.barrier()
        if mid_hook is not None:
            mid_hook()
        with ExitStack() as st:
            A = [self.sb(st, "g_A%d" % d, [CHS, NCH, 8], F32) for d in range(2)]
            U = [self.sb(st, "g_U%d" % d, [CHS, NCH, 8], F32) for d in range(2)]
            AE = [self.sb(st, "g_AE%d" % d, [128, NCH, 8], F32) for d in range(2)]
            tri = self.sb(st, "g_tri", [CHS, 2, CHS], F32)
            onesf = self.sb(st, "g_ones", [CHS, 128], F32)
            with ExitStack() as st2:
                G = self.sb(st2, "g_G", [CHS, NCH, 32], F32)
                LF = self.sb(st2, "g_LF", [CHS, 2, NCH, 8], F32)
                tmp = self.sb(st2, "g_tmp", [CHS, 512], F32)
                self.dma("sp", G[:], self.GATES.ap().rearrange("(c p) g -> p c g", p=CHS), reads=["GATES"], writes=["g_G"])
                self.dma("sp", tri[:], self.tri_in.ap(), writes=["g_tri"])
                s.op("pool", lambda e: e.memset(onesf[:], 1.0), writes=["g_ones"])
                for d in range(2):
                    fc = 8 + 16 * d
                    s.op("act", lambda e, d=d, fc=fc: e.activation(out=LF[:, d, :, :], in_=G[:, :, fc:fc + 8],
                                                                   func=AF.Exp, scale=-1.0),
                         reads=["g_G"], writes=[("g_LF", d, d + 1)])
                    s.op("act", lambda e, d=d: e.activation(out=LF[:, d, :, :], in_=LF[:, d, :, :], func=AF.Ln,
                                                            scale=1.0, bias=1.0),
                         reads=[("g_LF", d, d + 1)], writes=[("g_LF", d, d + 1)])
                    s.op("dve", lambda e, d=d: e.tensor_scalar(out=LF[:, d, :, :], in0=LF[:, d, :, :], scalar1=-1.0,
                                                               scalar2=None, op0=ALU.mult),
                         reads=[("g_LF", d, d + 1)], writes=[("g_LF", d, d + 1)])
                CH = 60
                for d in range(2):
                    ic = 16 * d
                    for c0 in range(0, NCH, CH):
                        c1 = min(NCH, c0 + CH)
                        ncol = (c1 - c0) * 8
                        rhs = LF[:, d, c0:c1, :].rearrange("p c h -> p (c h)")
                        s.op("pe", lambda e, d=d, rhs=rhs, ncol=ncol: e.matmul(self.ps[0][0:CHS, 0:ncol], lhsT=tri[:, d, :],
                                                                              rhs=rhs, start=True, stop=True),
                             reads=["g_LF", "g_tri"], writes=["ps0"])
                        s.op("pe", lambda e, rhs=rhs, ncol=ncol: e.matmul(self.ps[1][:, 0:ncol], lhsT=onesf[:], rhs=rhs,
                                                                          start=True, stop=True),
                             reads=["g_LF", "g_ones"], writes=["ps1"])
                        s.op("act", lambda e, d=d, c0=c0, c1=c1, ncol=ncol: e.activation(
                            out=A[d][:, c0:c1, :].rearrange("p c h -> p (c h)"), in_=self.ps[0][0:CHS, 0:ncol], func=AF.Exp),
                            reads=["ps0"], writes=["g_A%d" % d])
                        s.op("dve", lambda e, d=d, c0=c0, c1=c1, ncol=ncol, ic=ic: e.scalar_tensor_tensor(
                            out=tmp[:, 0:ncol].rearrange("p (c h) -> p c h", h=8),
                            in0=self.ps[0][0:CHS, 0:ncol].rearrange("p (c h) -> p c h", h=8), scalar=-1.0,
                            in1=G[:, c0:c1, ic:ic + 8], op0=ALU.mult, op1=ALU.add),
                            reads=["ps0", "g_G"], writes=["g_tmp"])
                        s.op("act", lambda e, d=d, c0=c0, c1=c1, ncol=ncol: e.activation(
                            out=U[d][:, c0:c1, :].rearrange("p c h -> p (c h)"), in_=tmp[:, 0:ncol], func=AF.Exp),
                            reads=["g_tmp"], writes=["g_U%d" % d])
                        s.op("act", lambda e, d=d, c0=c0, c1=c1, ncol=ncol: e.activation(
                            out=AE[d][:, c0:c1, :].rearrange("p c h -> p (c h)"), in_=self.ps[1][:, 0:ncol], func=AF.Exp),
                            reads=["ps1"], writes=["g_AE%d" % d])
            s.barrier()
            GS = 256 // CHS
            qT = [[self.sb(st, "s_qT%d%d" % (d, i), [128, 8, GS * CHS], BF16) for i in range(2)] for d in range(2)]
            kT = [[self.sb(st, "s_kT%d%d" % (d, i), [128, 8, GS * CHS], BF16) for i in range(2)] for d in range(2)]
            kk = [[self.sb(st, "s_kk%d%d" % (d, i), [CHS, 8, 128], BF16) for i in range(2)] for d in range(2)]
            ku = [[self.sb(st, "s_ku%d%d" % (d, i), [CHS, 8, 128], BF16) for i in range(2)] for d in range(2)]
            va = [[self.sb(st, "s_va%d%d" % (d, i), [CHS, 8, 257], BF16) for i in range(2)] for d in range(2)]
            stm = [[self.sb(st, "s_stm%d%d" % (d, i), [CHS, 8, CHS], BF16) for i in range(2)] for d in range(2)]
            hbuf = [[self.sb(st, "s_hb%d%d" % (d, i), [CHS, 8, 256], F32) for i in range(2)] for d in range(2)]
            Dst = [self.sb(st, "s_D%d" % d, [128, 8, 257], F32) for d in range(2)]
            Cb = [[self.sb(st, "s_Cb%d%d" % (d, i), [128, 8, 257], BF16) for i in range(2)] for d in range(2)]
            r8 = [self.sb(st, "s_r8%d" % d, [CHS, 8], F32) for d in range(2)]
            r8n = [self.sb(st, "s_r8n%d" % d, [CHS, 8], F32) for d in range(2)]
            for d in range(2):
                s.op("pool", lambda e, d=d: e.memset(Dst[d][:], 0.0), writes=["s_D%d" % d])
                s.op("pool", lambda e, d=d: e.memset(Cb[d][0][:], 0.0), writes=["s_Cb%d0" % d])
                for i in range(2):
                    s.op("pool", lambda e, d=d, i=i: e.memset(va[d][i][:, :, 256:257], 1.0),
                         writes=[("s_va%d%d" % (d, i), 256, 257)])
            order = [list(range(NCH)), list(range(NCTX - 1, -1, -1)) + list(range(NCH - 1, NCTX - 1, -1))]
            gorder = []
            for d in range(2):
                go = []
                for c in order[d]:
                    if not go or go[-1] != c // GS:
                        go.append(c // GS)
                gorder.append(go)
            gstate = [dict(pos=-1, loaded=0), dict(pos=-1, loaded=0)]
            HO = [self.HF, self.HB]

            def load_group(d, gi):
                g = gorder[d][gi]
                a = gi % 2
                self.dma("sp", qT[d][a][:], self.MQT.ap()[:, :, g * GS * CHS:(g + 1) * GS * CHS], reads=["MQT"],
                         writes=["s_qT%d%d" % (d, a)])
                self.dma("sp", kT[d][a][:], self.MKT.ap()[:, :, g * GS * CHS:(g + 1) * GS * CHS], reads=["MKT"],
                         writes=["s_kT%d%d" % (d, a)])

            def scan_step(d, si):
                c = order[d][si]
                prev_c = order[d][si - 1] if si > 0 else None
                gs_ = gstate[d]
                if gs_["pos"] < 0 or gorder[d][gs_["pos"]] != c // GS:
                    gs_["pos"] += 1
                    while gs_["loaded"] <= min(gs_["pos"] + 1, len(gorder[d]) - 1):
                        load_group(d, gs_["loaded"])
                        gs_["loaded"] += 1
                ga = gs_["pos"] % 2
                co = (c % GS) * CHS
                a = si % 2
                nm = "%d%d" % (d, a)
                t0 = c * CHS
                self.dma("sp", kk[d][a][:], self.MK.ap()[t0:t0 + CHS, :].rearrange("p (h e) -> p h e", h=8), reads=["MK"],
                         writes=["s_kk" + nm])
                self.dma("sp", va[d][a][:, :, 0:256], self.MV.ap()[t0:t0 + CHS, :].rearrange("p (h e) -> p h e", h=8),
                         reads=["MV"], writes=[("s_va" + nm, 0, 256)])
                for hh in range(2):
                    for h4 in range(4):
                        h = hh * 4 + h4
                        s.op("pe", lambda e, h=h, h4=h4: e.matmul(self.ps[4][0:CHS, h4 * CHS:(h4 + 1) * CHS],
                                                                 lhsT=kT[d][ga][:, h, co:co + CHS],
                                                                 rhs=qT[d][ga][:, h, co:co + CHS], start=True, stop=True),
                             reads=["s_kT%d%d" % (d, ga), "s_qT%d%d" % (d, ga)], writes=["ps4"])
                    for h4 in range(4):
                        h = hh * 4 + h4
                        s.op("dve", lambda e, h=h, h4=h4: e.scalar_tensor_tensor(
                            out=stm[d][a][:, h, :], in0=self.ps[4][0:CHS, h4 * CHS:(h4 + 1) * CHS],
                            scalar=U[d][:, c, h:h + 1], in1=tri[:, d, :], op0=ALU.mult, op1=ALU.mult),
                            reads=["ps4", "g_tri"], writes=[("s_stm" + nm, h, h + 1)])
                for h in range(8):
                    s.op("pool", lambda e, h=h: e.tensor_scalar(
                        out=ku[d][a][:, h, :], in0=kk[d][a][:, h, :], scalar1=U[d][:, c, h:h + 1], scalar2=1.0,
                        op0=ALU.mult, op1=ALU.mult),
                        reads=["s_kk" + nm], writes=[("s_ku" + nm, h, h + 1)])
                if si + 1 < len(order[d]):
                    for h in range(8):
                        pb = 6 + (h % 2)
                        s.op("pe", lambda e, h=h, pb=pb: e.matmul(
                            self.ps[pb][:, 0:257], lhsT=ku[d][a][:, h, :], rhs=va[d][a][:, h, :], start=True, stop=True),
                            reads=[("s_ku" + nm, h, h + 1), "s_va" + nm], writes=["ps%d" % pb])
                        if prev_c is None:
                            s.op("dve", lambda e, h=h, pb=pb: e.tensor_copy(out=Dst[d][:, h, :], in_=self.ps[pb][:, 0:257]),
                                 reads=["ps%d" % pb], writes=[("s_D%d" % d, h, h + 1)])
                        else:
                            s.op("dve", lambda e, h=h, pb=pb: e.scalar_tensor_tensor(
                                out=Dst[d][:, h, :], in0=Dst[d][:, h, :], scalar=AE[d][:, prev_c, h:h + 1],
                                in1=self.ps[pb][:, 0:257], op0=ALU.mult, op1=ALU.add),
                                reads=["ps%d" % pb, ("s_D%d" % d, h, h + 1)], writes=[("s_D%d" % d, h, h + 1)])
                        s.op("pool", lambda e, h=h: e.tensor_scalar(
                            out=Cb[d][(si + 1) % 2][:, h, :], in0=Dst[d][:, h, :], scalar1=AE[d][:, c, h:h + 1],
                            scalar2=1.0, op0=ALU.mult, op1=ALU.mult),
                             reads=[("s_D%d" % d, h, h + 1)], writes=[("s_Cb%d%d" % (d, (si + 1) % 2), h, h + 1)])

                for h in range(8):
                    pb = h // 2
                    o_ = (h % 2) * 256
                    s.op("pe", lambda e, h=h, pb=pb, o_=o_: e.matmul(
                        self.ps[pb][0:CHS, o_:o_ + 256], lhsT=qT[d][ga][:, h, co:co + CHS], rhs=Cb[d][si % 2][:, h, 0:256],
                        start=True, stop=False),
                        reads=["s_qT%d%d" % (d, ga), ("s_Cb%d%d" % (d, si % 2), h, h + 1)], writes=["ps%d" % pb])
                    s.op("pe", lambda e, h=h, pb=pb, o_=o_: e.matmul(
                        self.ps[pb][0:CHS, o_:o_ + 256], lhsT=stm[d][a][:, h, :], rhs=va[d][a][:, h, 0:256],
                        start=False, stop=True),
                        reads=[("s_stm" + nm, h, h + 1), "s_va" + nm], writes=["ps%d" % pb])
                for h in range(8):
                    s.op("pe", lambda e, h=h: e.matmul(
                        self.ps[5][0:CHS, h:h + 1], lhsT=qT[d][ga][:, h, co:co + CHS], rhs=Cb[d][si % 2][:, h, 256:257],
                        start=True, stop=False),
                        reads=["s_qT%d%d" % (d, ga), ("s_Cb%d%d" % (d, si % 2), h, h + 1)], writes=["ps5"])
                    s.op("pe", lambda e, h=h: e.matmul(
                        self.ps[5][0:CHS, h:h + 1], lhsT=stm[d][a][:, h, :], rhs=va[d][a][:, h, 256:257],
                        start=False, stop=True),
                        reads=[("s_stm" + nm, h, h + 1), "s_va" + nm], writes=["ps5"])
                rr = r8[d]
                rn = "s_r8%d" % d
                s.op("dve", lambda e: e.scalar_tensor_tensor(out=rr[:], in0=self.ps[5][0:CHS, 0:8], scalar=1.0,
                                                             in1=A[d][:, c, :], op0=ALU.mult, op1=ALU.mult),
                     reads=["ps5"], writes=[rn])
                rneg = r8n[d]
                s.op("dve", lambda e: e.tensor_scalar(out=rneg[:], in0=rr[:], scalar1=-1.0, scalar2=None, op0=ALU.mult),
                     reads=[rn], writes=[rn + "n"])
                s.op("dve", lambda e: e.scalar_tensor_tensor(out=rr[:], in0=rr[:], scalar=1.0, in1=rneg[:],
                                                             op0=ALU.max, op1=ALU.max),
                     reads=[rn, rn + "n"], writes=[rn])
                s.op("dve", lambda e: e.reciprocal(out=rr[:], in_=rr[:]), reads=[rn], writes=[rn])
                s.op("dve", lambda e: e.tensor_tensor(out=rr[:], in0=rr[:], in1=A[d][:, c, :], op=ALU.mult),
                     reads=[rn], writes=[rn])
                for h in range(8):
                    pb = h // 2
                    o_ = (h % 2) * 256
                    s.op("act", lambda e, h=h, pb=pb, o_=o_: e.activation(
                        out=hbuf[d][a][:, h, :], in_=self.ps[pb][0:CHS, o_:o_ + 256], func=AF.Copy, scale=rr[:, h:h + 1]),
                        reads=["ps%d" % pb, rn], writes=[("s_hb" + nm, h, h + 1)])
                self.dma("sp", HO[d].ap()[t0:t0 + CHS, :], hbuf[d][a][:].rearrange("p h e -> p (h e)"),
                         reads=["s_hb" + nm], writes=[("H%d" % d, t0, t0 + CHS)])
            for si in range(NCH):
                for d in range(2):
                    scan_step(d, si)
        s.barrier()
        with ExitStack() as st:
            mwo = self.sb(st, "o_wo", [128, KC, D], BF16)
            gh = self.sb(st, "o_gh", [128, D], F32)
            idb = self.sb(st, "o_idb", [128, 128], BF16)
            hf = [self.sb(st, "o_hf%d" % i, [128, D], F32) for i in range(2)]
            hbk = [self.sb(st, "o_hbk%d" % i, [128, D], F32) for i in range(2)]
            so = [self.sb(st, "o_so%d" % i, [128, D], BF16) for i in range(2)]
            sqh = self.sb(st, "o_sq", [128, D], F32)
            ms = [self.sb(st, "o_ms%d" % i, [128, 8], F32) for i in range(2)]
            yb = [self.sb(st, "o_yb%d" % i, [128, D], BF16) for i in range(2)]
            yT = [self.sb(st, "o_yT%d" % i, [128, KC, 128], BF16) for i in range(2)]
            xr = [self.sb(st, "o_xr%d" % i, [128, KC, 128], F32) for i in range(2)]
            xo = [self.sb(st, "o_xo%d" % i, [128, KC, 128], F32) for i in range(2)]
            wo_v = self.mwo_b[j].ap().rearrange("(k p) f -> p k f", p=128)
            for kc in range(KC):
                self.dma("sp", mwo[:, kc, :], wo_v[:, kc, :], writes=["o_wo"], extra=self.bg[("mwo", j)])
            self.dma("sp", gh[:], self.ghead_in.ap()[j], writes=["o_gh"])
            s.op("act", lambda e: e.activation(out=idb[:], in_=self.ident[:], func=AF.Copy), reads=["ident"],
                 writes=["o_idb"])
            psb16 = [p.bitcast(BF16) for p in self.ps]

            def m4_block(b):
                a = b % 2
                t0 = b * 128
                sidx = 1 if t0 < CT else 0
                n_ = "%d" % a
                self.dma("sp", hf[a][:], self.HF.ap()[t0:t0 + 128, :], reads=[("H0", t0, t0 + 128)], writes=["o_hf" + n_])
                self.dma("sp", hbk[a][:], self.HB.ap()[t0:t0 + 128, :], reads=[("H1", t0, t0 + 128)], writes=["o_hbk" + n_])
                self.dma("sp", so[a][:], self.SIGO.ap()[t0:t0 + 128, :], reads=[("SIGO", t0, t0 + 128)], writes=["o_so" + n_])
                s.op("dve", lambda e: e.tensor_tensor(out=hf[a][:], in0=hf[a][:], in1=hbk[a][:], op=ALU.add),
                     reads=["o_hf" + n_, "o_hbk" + n_], writes=["o_hf" + n_])
                s.op("act", lambda e: e.activation(out=sqh[:], in_=hf[a][:], func=AF.Square),
                     reads=["o_hf" + n_], writes=["o_sq"])
                s.op("dve", lambda e: e.tensor_reduce(out=ms[a][:], in_=sqh[:].rearrange("p (h e) -> p h e", h=8),
                                                      axis=mybir.AxisListType.X, op=ALU.add),
                     reads=["o_sq"], writes=["o_ms" + n_])
                s.op("act", lambda e: e.activation(out=ms[a][:], in_=ms[a][:], func=AF.Sqrt, scale=1.0 / 256, bias=EPS),
                     reads=["o_ms" + n_], writes=["o_ms" + n_])
                s.op("dve", lambda e: e.reciprocal(out=ms[a][:], in_=ms[a][:]), reads=["o_ms" + n_], writes=["o_ms" + n_])
                s.op("pool", lambda e: e.tensor_tensor(out=hbk[a][:], in0=so[a][:], in1=gh[:], op=ALU.mult),
                     reads=["o_so" + n_, "o_gh", "o_hbk" + n_], writes=["o_hbk" + n_])
                for h in range(8):
                    s.op("dve", lambda e, h=h: e.scalar_tensor_tensor(
                        out=yb[a][:, h * 256:(h + 1) * 256], in0=hf[a][:, h * 256:(h + 1) * 256], scalar=ms[a][:, h:h + 1],
                        in1=hbk[a][:, h * 256:(h + 1) * 256], op0=ALU.mult, op1=ALU.mult),
                        reads=["o_hf" + n_, "o_hbk" + n_, "o_ms" + n_], writes=[("o_yb" + n_, h, h + 1)])
                for q in range(2):
                    pb = (b * 2 + q) % 2
                    for r in range(8):
                        kc = q * 8 + r
                        s.op("pe", lambda e, kc=kc, r=r, pb=pb: e.transpose(
                            psb16[pb][:, r * 128:(r + 1) * 128], yb[a][:, kc * 128:(kc + 1) * 128], idb[:]),
                            reads=["o_yb" + n_, "o_idb"], writes=["ps%d" % pb])
                    if q == 0:
                        s.op("act", lambda e, pb=pb: e.activation(
                            out=yT[a][:, 0:8, :].rearrange("p k t -> p (k t)"), in_=psb16[pb][:, :], func=AF.Copy),
                            reads=["ps%d" % pb], writes=[("o_yT" + n_, 0, 8)])
                    else:
                        s.op("dve", lambda e, pb=pb: e.tensor_copy(
                            out=yT[a][:, 8:16, :].rearrange("p k t -> p (k t)"), in_=psb16[pb][:, :]),
                            reads=["ps%d" % pb], writes=[("o_yT" + n_, 8, 16)])
            def m4_back(b):
                a = b % 2
                t0 = b * 128
                sidx = 1 if t0 < CT else 0
                n_ = "%d" % a
                self.dma("sp", xr[a][:], XTv[:, :, t0:t0 + 128], reads=[("XT", t0, t0 + 128)], writes=["o_xr" + n_])
                for q4 in range(4):
                    pb = 2 + (b * 4 + q4) % 4
                    for r in range(4):
                        dc = q4 * 4 + r
                        for kc in range(KC):
                            s.op("pe", lambda e, dc=dc, kc=kc, r=r, pb=pb: e.matmul(
                                self.ps[pb][:, r * 128:(r + 1) * 128], lhsT=mwo[:, kc, dc * 128:(dc + 1) * 128],
                                rhs=yT[a][:, kc, :], start=(kc == 0), stop=(kc == KC - 1)),
                                reads=["o_wo", "o_yT" + n_], writes=["ps%d" % pb])
                    for r in range(4):
                        dc = q4 * 4 + r
                        s.op("dve", lambda e, dc=dc, r=r, pb=pb: e.scalar_tensor_tensor(
                            out=xo[a][:, dc, :], in0=self.ps[pb][:, r * 128:(r + 1) * 128],
                            scalar=self.MOD[:, l, 32 + dc, sidx:sidx + 1], in1=xr[a][:, dc, :],
                            op0=ALU.mult, op1=ALU.add),
                            reads=["ps%d" % pb, "o_xr" + n_], writes=[("o_xo" + n_, dc, dc + 1)])
                self.dma("sp", XTv[:, :, t0:t0 + 128], xo[a][:], reads=["o_xo" + n_], writes=[("XT", t0, t0 + 128)])

            last_full = self.plan_is_full() and l == self.depth - 1
            blks = [b for b in range(NB) if not (last_full and b < CT // 128)]
            for i, b in enumerate(blks):
                m4_block(b)
                if i > 0:
                    m4_back(blks[i - 1])
            m4_back(blks[-1])
        s.barrier()

    def plan_is_full(self):
        return getattr(self, "full", False)

    def final(self):
        nc, s = self.nc, self.s
        L = self.L
        XTv = self.XTv()
        if self.xt_dbg is not None:
            self.dma("sp", self.xt_dbg.ap(), self.XT.ap(), reads=["XT"], writes=["xt_dbg"])
        with ExitStack() as st:
            xt = [self.sb(st, "z_xt%d" % i, [128, KC, 128], F32) for i in range(2)]
            sq = [self.sb(st, "z_sq%d" % i, [128, KC, 128], BF16) for i in range(2)]
            scr = [self.sb(st, "z_scr%d" % i, [128, 128], F32) for i in range(2)]
            yo = [self.sb(st, "z_yo%d" % i, [128, D], F32) for i in range(2)]
            def fin_front(bi):
                a = bi % 2
                t0 = CT + bi * 128
                xn, qn, sn, yn = "z_xt%d" % a, "z_sq%d" % a, "z_scr%d" % a, "z_yo%d" % a
                self.dma("sp", xt[a][:], XTv[:, :, t0:t0 + 128], reads=[("XT", t0, t0 + 128)], writes=[xn])
                s.op("act", lambda e, a=a: e.activation(out=sq[a][:], in_=xt[a][:], func=AF.Square),
                     reads=[xn], writes=[qn])
                for kc in range(KC):
                    s.op("pe", lambda e, a=a, kc=kc: e.matmul(self.ps[6][:, 0:128], lhsT=self.ones_b[:],
                                                              rhs=sq[a][:, kc, :], start=(kc == 0),
                                                              stop=(kc == KC - 1)),
                         reads=[qn, "ones_b"], writes=["ps6"])
                s.op("act", lambda e, a=a: e.activation(out=scr[a][:], in_=self.ps[6][:, 0:128], func=AF.Sqrt,
                                                        scale=1.0 / D, bias=EPS), reads=["ps6"], writes=[sn])
                s.op("dve", lambda e, a=a: e.reciprocal(out=scr[a][:], in_=scr[a][:]), reads=[sn], writes=[sn])
                for kc in range(KC):
                    me = "dve" if kc % 2 == 0 else "pool"
                    if me == "dve":
                        s.op("dve", lambda e, a=a, kc=kc: e.scalar_tensor_tensor(
                            out=xt[a][:, kc, :], in0=xt[a][:, kc, :], scalar=self.gfin[:, kc:kc + 1],
                            in1=scr[a][:], op0=ALU.mult, op1=ALU.mult),
                            reads=[(xn, kc, kc + 1), sn, "gfin"], writes=[(xn, kc, kc + 1)])
                    else:
                        s.op("pool", lambda e, a=a, kc=kc: e.tensor_tensor(
                            out=xt[a][:, kc, :], in0=xt[a][:, kc, :], in1=scr[a][:], op=ALU.mult),
                            reads=[(xn, kc, kc + 1), sn], writes=[(xn, kc, kc + 1)])
                        s.op("pool", lambda e, a=a, kc=kc: e.tensor_scalar(
                            out=xt[a][:, kc, :], in0=xt[a][:, kc, :], scalar1=self.gfin[:, kc:kc + 1], scalar2=1.0,
                            op0=ALU.mult, op1=ALU.mult),
                            reads=[(xn, kc, kc + 1), "gfin"], writes=[(xn, kc, kc + 1)])
            def fin_back(bi):
                a = bi % 2
                xn, qn, sn, yn = "z_xt%d" % a, "z_sq%d" % a, "z_scr%d" % a, "z_yo%d" % a
                for q in range(4):
                    pb = (bi * 4 + q) % 6
                    for r in range(4):
                        kc = q * 4 + r
                        s.op("pe", lambda e, a=a, kc=kc, pb=pb, r=r: e.transpose(
                            self.ps[pb][:, r * 128:(r + 1) * 128], xt[a][:, kc, :], self.ident[:]),
                            reads=[(xn, kc, kc + 1), "ident"], writes=["ps%d" % pb])
                    if q % 2 == 0:
                        s.op("act", lambda e, a=a, q=q, pb=pb: e.activation(
                            out=yo[a][:, q * 512:(q + 1) * 512], in_=self.ps[pb][:, :], func=AF.Copy),
                            reads=["ps%d" % pb], writes=[(yn, q, q + 1)])
                    else:
                        s.op("dve", lambda e, a=a, q=q, pb=pb: e.tensor_copy(
                            out=yo[a][:, q * 512:(q + 1) * 512], in_=self.ps[pb][:, :]),
                            reads=["ps%d" % pb], writes=[(yn, q, q + 1)])
                self.dma("sp", self.out.ap()[bi * 128:(bi + 1) * 128, :], yo[a][:], reads=[yn], writes=["out"])

            nb_ = L // 128
            for bi in range(nb_):
                fin_front(bi)
                if bi > 0:
                    fin_back(bi - 1)
            fin_back(nb_ - 1)
        s.barrier()


def lay_pk(v):
    sh = v.shape[:-1]
    return np.ascontiguousarray(np.swapaxes(v.reshape(sh + (-1, 128)), -1, -2))


def host_inputs(inp, b, depth):
    f = np.float32
    m = {}
    m["x"] = np.ascontiguousarray(inp["x"][b], dtype=f)
    m["ctx"] = np.ascontiguousarray(inp["ctx"][b], dtype=f)
    m["cc"] = np.ascontiguousarray(np.stack([lay_pk(inp["c"][b]), lay_pk(inp["c_ctx"])], axis=-1), dtype=f)
    bm = lay_pk(inp["b_mod"][:depth])
    m["b_mod"] = np.ascontiguousarray(np.repeat(bm[..., None], 2, axis=-1))
    m["g_mix"] = np.ascontiguousarray(np.repeat(lay_pk(inp["g_mix"][:depth])[..., None], 2, axis=-1))
    m["g_ffn"] = np.ascontiguousarray(np.repeat(lay_pk(inp["g_ffn"][:depth])[..., None], 2, axis=-1))
    m["g_final"] = lay_pk(inp["g_final"])
    m["conv_w"] = np.ascontiguousarray(np.transpose(lay_pk(inp["ffn_conv_w"][:depth]), (0, 2, 3, 1)))
    m["conv_b"] = lay_pk(inp["ffn_conv_b"][:depth])
    m["ident"] = np.eye(128, dtype=f)
    na = (depth + 1) // 2
    L = m["x"].shape[0]
    T = CT + L
    m["w_qkv"] = inp["attn_w_qkv"][:max(na, 1)]
    m["w_o"] = inp["attn_w_o"][:max(na, 1)]
    sk = inp["attn_sink"][:max(na, 1)]
    m["sink"] = np.ascontiguousarray(np.broadcast_to(sk[:, None, :, None], (sk.shape[0], 128, 16, 128)), dtype=f)
    nm_ = depth // 2
    m["w_in"] = inp["mlstm_w_in"][:max(nm_, 1)]
    m["m_w_o"] = inp["mlstm_w_o"][:max(nm_, 1)]
    bi = inp["mlstm_b_in"][:max(nm_, 1)]
    m["m_brow"] = np.ascontiguousarray(np.broadcast_to(bi[:, None, 1024:6176], (bi.shape[0], 128, 5152)), dtype=f)
    m["m_bqk"] = lay_pk(bi[:, 0:2048])
    gh_ = inp["mlstm_g_head"][:max(nm_, 1)]
    m["g_head"] = np.ascontiguousarray(np.broadcast_to(gh_[:, None, :], (gh_.shape[0], 128, D)), dtype=f)
    ss_ = np.arange(128)[:, None]
    tt_ = np.arange(128)[None, :]
    m["tri"] = np.ascontiguousarray(np.stack([(ss_ <= tt_), (ss_ >= tt_)], axis=1).astype(f))
    tok = np.arange(L)
    row = (tok // 64).astype(np.float64)
    col = (tok % 64).astype(np.float64)
    inv = 10000.0 ** (-np.arange(32, dtype=np.float64) / 32)
    ang = np.concatenate([row[:, None] * inv, col[:, None] * inv], axis=-1)
    ang = np.concatenate([ang, ang], axis=-1).astype(f)
    cosT = np.ones((128, T), f)
    sinT = np.zeros((128, T), f)
    cosT[:, CT:] = np.cos(ang).T
    sinT[:, CT:] = np.sin(ang).T
    m["cosT"] = cosT
    m["sinT"] = sinT
    rp = np.zeros((128, 128), f)
    for do in range(128):
        if do < 64:
            rp[do + 64, do] = -1.0
        else:
            rp[do - 64, do] = 1.0
    m["rperm"] = rp
    kj = np.arange(128)[:, None]
    qi = np.arange(128)[None, :]
    mprev = np.where(kj >= qi, 0.0, -1e30).astype(f)
    mnext = np.where(kj <= qi, 0.0, -1e30).astype(f)
    m["mask"] = np.ascontiguousarray(np.stack([np.tile(mprev, (1, 4)), np.tile(mnext, (1, 4))], axis=1))
    return m


_WCACHE = {}


def host_weights(inp, depth):
    wup = inp["ffn_w_up"][:depth]
    w = wup.reshape(depth, KC, 128, 2, NJ, 128)
    w = np.transpose(w, (0, 4, 2, 1, 3, 5))
    wup_b = np.ascontiguousarray(w).reshape(depth, NJ * 128, KC * 256)
    wdn = inp["ffn_w_down"][:depth]
    w = wdn.reshape(depth, NJ, 128, 16, 128)
    w = np.transpose(w, (0, 3, 2, 1, 4))
    wdn_b = np.ascontiguousarray(w).reshape(depth, 16 * 128, NJ * 128)
    w = inp["w_mod"][:depth].reshape(depth, KC, 128, 24, 512)
    wmod_b = np.ascontiguousarray(np.transpose(w, (0, 3, 2, 1, 4))).reshape(depth, 24 * 128, KC * 512)
    return {"w_up": wup_b, "w_down": wdn_b, "w_mod": wmod_b}


def kernel(**inputs):
    depth = 4
    L = 4096
    plan = [("att", 0), ("ffn", 0), ("mls", 1), ("ffn", 1), ("att", 2), ("ffn", 2), ("mls", 3), ("ffn", 3)]
    inputs = {k: np.asarray(v) for k, v in inputs.items()}
    P = Prog(L, plan, depth)
    P.full = True
    nc = P.build()
    shared = host_weights(inputs, depth)
    in_maps = []
    for b in range(8):
        m = host_inputs(inputs, b, depth)
        m.update(shared)
        in_maps.append(m)
    res = run_bass_kernel_spmd(nc, in_maps, core_ids=list(range(8)))
    return np.stack([np.asarray(r["out"], dtype=np.float32) for r in res.results], axis=0)
```

```python
import numpy as np
from contextlib import ExitStack
import concourse.bass as bass
import concourse.mybir as mybir
from concourse.bass_utils import run_bass_kernel_spmd

F32 = mybir.dt.float32
BF16 = mybir.dt.bfloat16
AF = mybir.ActivationFunctionType
ALU = mybir.AluOpType

D = 2048
KC = 16
DFF = 5632
NJ = 44
CT = 256
EPS = 1e-6
BIG = 1 << 40


class Op:
    __slots__ = ("eng", "fn", "deps", "needed", "semval", "is_dma", "dsem", "dval")


class Sched:
    CE = ("pe", "act", "dve", "pool", "sp")

    NPOOL = 12

    def __init__(self, nc, ndma=28):
        self.nc = nc
        self.ops = {e: [] for e in self.CE}
        self.wr = {}
        self.rd = {}
        self.ndma = ndma
        self.dma_cnt = [0] * ndma
        self.dma_last = [None] * ndma
        self.dma_rr = 0
        self.dma_rr_pool = 0
        self.last = {e: None for e in self.CE}
        self.pending = {e: set() for e in self.CE}
        self.nops = 0

    @staticmethod
    def _norm(r):
        if isinstance(r, str):
            return (r, 0, BIG)
        return r

    def op(self, eng, fn, reads=(), writes=(), dma=False, extra=()):
        o = Op()
        o.eng = eng
        o.fn = fn
        o.needed = False
        o.semval = 0
        o.is_dma = dma
        o.dsem = -1
        o.dval = 0
        deps = set(extra)
        rds = [self._norm(r) for r in reads if r is not None]
        wrs = [self._norm(r) for r in writes if r is not None]
        wrs += [r for r in rds if r[0].startswith("ps")]
        rds = [r for r in rds if not r[0].startswith("ps")]
        for (name, lo, hi) in rds:
            for (l, h, w) in self.wr.get(name, ()):
                if l < hi and lo < h:
                    deps.add(w)
        for (name, lo, hi) in wrs:
            for (l, h, w) in self.wr.get(name, ()):
                if l < hi and lo < h:
                    deps.add(w)
            for (l, h, w) in self.rd.get(name, ()):
                if l < hi and lo < h:
                    deps.add(w)
        for (name, lo, hi) in rds:
            lst = self.rd.setdefault(name, [])
            if not dma:
                lst[:] = [t for t in lst if not (t[2].eng == eng and not t[2].is_dma and lo <= t[0] and t[1] <= hi)]
            lst.append((lo, hi, o))
        for (name, lo, hi) in wrs:
            lst = self.wr.setdefault(name, [])
            lst[:] = [t for t in lst if not (lo <= t[0] and t[1] <= hi)]
            lst.append((lo, hi, o))
            lst2 = self.rd.get(name)
            if lst2:
                lst2[:] = [t for t in lst2 if not (lo <= t[0] and t[1] <= hi)]
        deps |= self.pending[eng]
        self.pending[eng] = set()
        deps.discard(o)
        if eng == "pe" and not dma:
            deps = {d for d in deps if not (d.eng == "pe" and not d.is_dma)}
        for d in deps:
            d.needed = True
        o.deps = deps
        if dma:
            if eng == "pool":
                k = self.ndma - self.NPOOL + self.dma_rr_pool
                self.dma_rr_pool = (self.dma_rr_pool + 1) % self.NPOOL
            else:
                k = self.dma_rr
                self.dma_rr = (k + 1) % (self.ndma - self.NPOOL)
            if self.dma_last[k] is not None:
                self.dma_last[k].needed = True
                o.deps.add(self.dma_last[k])
            self.dma_cnt[k] += 1
            o.dsem = k
            o.dval = 16 * self.dma_cnt[k]
            self.dma_last[k] = o
        else:
            self.last[eng] = o
        self.ops[eng].append(o)
        self.nops += 1
        return o

    def barrier(self, final=False):
        deps = set()
        for e in self.CE:
            if self.last[e] is not None:
                deps.add(self.last[e])
        for k in range(self.ndma if final else self.ndma - self.NPOOL):
            if self.dma_last[k] is not None:
                deps.add(self.dma_last[k])
        for d in deps:
            d.needed = True
        for e in self.CE:
            self.pending[e] = set(deps)
        self.wr = {}
        self.rd = {}

    def emit(self, es):
        nc = self.nc
        csem = {e: es.enter_context(nc.semaphore("c_" + e)) for e in self.CE}
        dsem = [es.enter_context(nc.semaphore("d%d" % k)) for k in range(self.ndma)]
        self.barrier(final=True)
        finals = {e: self.pending[e] for e in self.CE}
        for e in self.CE:
            cnt = 0
            for o in self.ops[e]:
                if o.needed and not o.is_dma:
                    cnt += 1
                    o.semval = cnt

        self.stats = {e: (len(self.ops[e]), max([o.semval for o in self.ops[e]] + [0])) for e in self.CE}
        self.stats['dma'] = max(self.dma_cnt) * 16
        def run(ename, h):
            known = {}

            def waits(deps):
                need = {}
                for d in deps:
                    if d.is_dma:
                        key, val = ("d", d.dsem), d.dval
                    else:
                        key, val = ("c", d.eng), d.semval
                    if need.get(key, 0) < val:
                        need[key] = val
                for key, val in need.items():
                    if known.get(key, 0) < val:
                        sem = dsem[key[1]] if key[0] == "d" else csem[key[1]]
                        h.wait_ge(sem, val)
                        known[key] = val

            for o in self.ops[ename]:
                waits(o.deps)
                inst = o.fn(h)
                if o.is_dma:
                    inst.then_inc(dsem[o.dsem], 16)
                elif o.needed:
                    inst.then_inc(csem[ename], 1)
            waits(finals[ename])

        block = es.enter_context(nc.Block())

        @block.tensor
        def _(h):
            run("pe", h)

        @block.scalar
        def _(h):
            run("act", h)

        @block.vector
        def _(h):
            run("dve", h)

        @block.gpsimd
        def _(h):
            run("pool", h)

        @block.sync
        def _(h):
            run("sp", h)


def seq_tiles(L, nt, base):
    o = [((k * L) // nt) // 2 * 2 for k in range(nt)] + [L]
    tiles = []
    for k in range(nt):
        m_lo = 0 if k == 0 else o[k] + 1
        m_hi = L if k == nt - 1 else o[k + 1] + 1
        tiles.append(dict(o_lo=base + o[k], o_hi=base + o[k + 1], m_lo=base + m_lo, m_hi=base + m_hi,
                          first=(k == 0), last=(k == nt - 1)))
    return tiles


class Prog:
    def __init__(self, L, plan, depth):
        self.L = L
        self.T = CT + L
        self.plan = plan
        self.depth = depth
        self.nc = bass.Bass("TRN2", target_bir_lowering=False)
        self.s = Sched(self.nc)
        self.es = ExitStack()
        self.bg = {}

    def din(self, name, shape, dt=F32):
        return self.nc.dram_tensor(name, list(shape), dt, kind="ExternalInput")

    def dscr(self, name, shape, dt):
        return self.nc.dram_tensor(name, list(shape), dt, kind="Internal")

    def sb(self, stack, name, shape, dt):
        self._uid = getattr(self, "_uid", 0) + 1
        return stack.enter_context(self.nc.sbuf_tensor("sb%d_%s" % (self._uid, name), list(shape), dt))

    def dma(self, eng, out, in_, reads=(), writes=(), extra=(), **kw):
        return self.s.op(eng, lambda h: h.dma_start(out=out, in_=in_, **kw), reads=reads, writes=writes,
                         dma=True, extra=extra)

    def cast_weights(self, key, dst, src, rows, cols):
        n = rows * cols
        assert n % 1024 == 0
        dv = dst.rearrange("r c -> (r c)").rearrange("(a b) -> a b", b=1024)
        sv = src.rearrange("r c -> (r c)").rearrange("(a b) -> a b", b=1024)
        nr = n // 1024
        ops = []
        r = 0
        while r < nr:
            r1 = min(nr, r + 8192)
            ops.append(self.dma("pool", dv[r:r1, :], sv[r:r1, :]))
            r = r1
        self.bg[key] = ops

    def build(self):
        nc, s, es = self.nc, self.s, self.es
        L, T, depth = self.L, self.T, self.depth
        NB = T // 128
        self.x_in = self.din("x", [L, D])
        self.ctx_in = self.din("ctx", [CT, D])
        self.cc_in = self.din("cc", [128, KC, 2])
        self.wmod_in = self.din("w_mod", [depth, 24 * 128, KC * 512])
        self.bmod_in = self.din("b_mod", [depth, 128, 96, 2])
        self.gmix_in = self.din("g_mix", [depth, 128, KC, 2])
        self.gffn_in = self.din("g_ffn", [depth, 128, KC, 2])
        self.gfin_in = self.din("g_final", [128, KC])
        self.wup_in = self.din("w_up", [depth, NJ * 128, KC * 256])
        self.wdn_in = self.din("w_down", [depth, 16 * 128, NJ * 128])
        self.cw_in = self.din("conv_w", [depth, 128, 88, 3])
        self.cb_in = self.din("conv_b", [depth, 128, 88])
        self.ident_in = self.din("ident", [128, 128])
        na = (depth + 1) // 2
        nm_ = depth // 2
        self.wqkv_in = self.din("w_qkv", [max(na, 1), D, 3072])
        self.wo_in = self.din("w_o", [max(na, 1), D, D])
        self.sink_in = self.din("sink", [max(na, 1), 128, 16, 128])
        self.win_in = self.din("w_in", [max(nm_, 1), D, 6176])
        self.mwo_in = self.din("m_w_o", [max(nm_, 1), D, D])
        self.mbrow_in = self.din("m_brow", [max(nm_, 1), 128, 5152])
        self.mbqk_in = self.din("m_bqk", [max(nm_, 1), 128, 16])
        self.ghead_in = self.din("g_head", [max(nm_, 1), 128, D])
        self.tri_in = self.din("tri", [128, 2, 128])
        self.cos_in = self.din("cosT", [128, T])
        self.sin_in = self.din("sinT", [128, T])
        self.rperm_in = self.din("rperm", [128, 128])
        self.mask_in = self.din("mask", [128, 2, 512])
        self.out = nc.dram_tensor("out", [L, D], F32, kind="ExternalOutput")
        self.xt_dbg = nc.dram_tensor("xt_dbg", [D, T], F32, kind="ExternalOutput") if getattr(self, "debug", False) else None
        self.XT = self.dscr("XT", [D, T], F32)
        self.wup_b = [self.dscr("wup_b%d" % l, [NJ * 128, KC * 256], BF16) for l in range(depth)]
        self.wdn_b = [self.dscr("wdn_b%d" % l, [16 * 128, NJ * 128], BF16) for l in range(depth)]
        self.wqkv_b = [self.dscr("wqkv_b%d" % i, [D, 3072], BF16) for i in range(na)]
        self.wo_b = [self.dscr("wo_b%d" % i, [D, D], BF16) for i in range(na)]
        self.QT = self.dscr("QT", [128, 16, T], BF16)
        self.KT = self.dscr("KT", [128, 4, T], BF16)
        self.V = self.dscr("V", [T, 512], BF16)
        self.win_b = [self.dscr("win_b%d" % i, [D, 6176], BF16) for i in range(nm_)]
        self.mwo_b = [self.dscr("mwo_b%d" % i, [D, D], BF16) for i in range(nm_)]
        self.MQT = self.dscr("MQT", [128, 8, T], BF16)
        self.MKT = self.dscr("MKT", [128, 8, T], BF16)
        self.MK = self.dscr("MK", [T, 1024], BF16)
        self.MV = self.dscr("MV", [T, D], BF16)
        self.SIGO = self.dscr("SIGO", [T, D], BF16)
        self.GATES = self.dscr("GATES", [T, 32], F32)
        self.HF = self.dscr("HF", [T, D], F32)
        self.HB = self.dscr("HB", [T, D], F32)
        self.ps = [es.enter_context(nc.psum_tensor("ps%d" % i, [128, 512], F32)) for i in range(8)]
        self.ident = self.sb(es, "ident", [128, 128], F32)
        self.ones_b = self.sb(es, "ones_b", [128, 128], BF16)
        self.MOD = self.sb(es, "MOD", [128, depth, 96, 2], F32)
        self.A1 = self.sb(es, "A1", [128, depth, KC, 2], F32)
        self.A2 = self.sb(es, "A2", [128, depth, KC, 2], F32)
        self.gfin = self.sb(es, "gfin", [128, KC], F32)

        self.dma("sp", self.ident[:], self.ident_in.ap(), writes=["ident"])
        self.dma("sp", self.gfin[:], self.gfin_in.ap(), writes=["gfin"])
        s.op("pool", lambda h: h.memset(self.ones_b[:], 1.0), writes=["ones_b"])
        self.prologue()
        for si_, step in enumerate(self.plan):
            kind, l = step
            hook = (lambda si_=si_: self.cast_step(si_ + 2))
            if kind == "ffn":
                self.ffn(l, mid_hook=hook)
            elif kind == "att":
                self.att(l, mid_hook=hook)
            elif kind == "mls":
                self.mls(l, mid_hook=hook)
        self.final()
        s.emit(es)
        es.close()
        return nc

    def XTv(self):
        return self.XT.ap().rearrange("(k p) t -> p k t", p=128)

    def cast_step(self, i):
        if i >= len(self.plan):
            return
        kind, l = self.plan[i]
        if kind == "mls":
            self.cast_weights(("win", l // 2), self.win_b[l // 2].ap(), self.win_in.ap()[l // 2], D, 6176)
            self.cast_weights(("mwo", l // 2), self.mwo_b[l // 2].ap(), self.mwo_in.ap()[l // 2], D, D)
        if kind == "att":
            self.cast_weights(("wqkv", l // 2), self.wqkv_b[l // 2].ap(), self.wqkv_in.ap()[l // 2], D, 3072)
            self.cast_weights(("wo", l // 2), self.wo_b[l // 2].ap(), self.wo_in.ap()[l // 2], D, D)
        if kind == "ffn":
            self.cast_weights(("wup", l), self.wup_b[l].ap(), self.wup_in.ap()[l], NJ * 128, KC * 256)
            self.cast_weights(("wdn", l), self.wdn_b[l].ap(), self.wdn_in.ap()[l], 16 * 128, NJ * 128)

    def prologue(self):
        nc, s = self.nc, self.s
        L, T, depth = self.L, self.T, self.depth
        self.cast_step(0)
        self.cast_step(1)
        with ExitStack() as st:
            xin = [self.sb(st, "p_xin%d" % i, [128, D], F32) for i in range(2)]
            xo = [self.sb(st, "p_xo%d" % i, [128, KC, 128], F32) for i in range(2)]
            blocks = [(self.ctx_in, i, i * 128) for i in range(CT // 128)] + \
                     [(self.x_in, i, CT + i * 128) for i in range(L // 128)]
            XTv = self.XTv()
            for bi, (src, i, t0) in enumerate(blocks):
                a = bi % 2
                self.dma("sp", xin[a][:], src.ap()[i * 128:(i + 1) * 128, :], writes=["xin%d" % a])
                for q in range(4):
                    pb = (bi * 4 + q) % 8
                    for r in range(4):
                        kc = q * 4 + r
                        s.op("pe", lambda h, a=a, kc=kc, pb=pb, r=r: h.transpose(
                            self.ps[pb][:, r * 128:(r + 1) * 128], xin[a][:, kc * 128:(kc + 1) * 128], self.ident[:]),
                            reads=["xin%d" % a, "ident"], writes=["ps%d" % pb])
                    eng = "act" if q % 2 == 0 else "dve"
                    if eng == "act":
                        s.op("act", lambda h, a=a, q=q, pb=pb: h.activation(
                            out=xo[a][:, q * 4:(q + 1) * 4, :], in_=self.ps[pb][:, :], func=AF.Copy),
                            reads=["ps%d" % pb], writes=[("xo%d" % a, q, q + 1)])
                    else:
                        s.op("dve", lambda h, a=a, q=q, pb=pb: h.tensor_copy(
                            out=xo[a][:, q * 4:(q + 1) * 4, :], in_=self.ps[pb][:, :]),
                            reads=["ps%d" % pb], writes=[("xo%d" % a, q, q + 1)])
                self.dma("sp", XTv[:, :, t0:t0 + 128], xo[a][:], reads=["xo%d" % a], writes=[("XT", t0, t0 + 128)])
        s.barrier()
        with ExitStack() as st:
            cc = self.sb(st, "p_cc", [128, KC, 2], F32)
            scc = self.sb(st, "p_scc", [128, KC, 2], F32)
            bm = self.sb(st, "p_bm", [128, depth, 96, 2], F32)
            gm = self.sb(st, "p_gm", [128, depth, KC, 2], F32)
            gf = self.sb(st, "p_gf", [128, depth, KC, 2], F32)
            wm = [self.sb(st, "p_wm%d" % i, [128, KC, 512], F32) for i in range(3)]
            modrow = self.sb(st, "p_modrow", [2, 6 * D], F32)
            self.dma("sp", cc[:], self.cc_in.ap(), writes=["cc"])
            self.dma("sp", bm[:], self.bmod_in.ap().rearrange("l p c s -> p l c s"), writes=["bm"])
            self.dma("sp", gm[:], self.gmix_in.ap().rearrange("l p c s -> p l c s"), writes=["gm"])
            self.dma("sp", gf[:], self.gffn_in.ap().rearrange("l p c s -> p l c s"), writes=["gf"])
            s.op("act", lambda h: h.activation(out=scc[:], in_=cc[:], func=AF.Silu), reads=["cc"], writes=["scc"])
            layers = sorted(set(l for (_, l) in self.plan))
            n = 0
            for l in layers:
                wv = self.wmod_in.ap()[l].rearrange("(t p) f -> t p f", p=128)
                for ct in range(24):
                    a = n % 3
                    pb = 1 + n % 2
                    n += 1
                    self.dma("sp", wm[a][:], wv[ct].rearrange("p (k f) -> p k f", k=KC), writes=["wm%d" % a])
                    for kc in range(KC):
                        s.op("pe", lambda h, a=a, kc=kc, pb=pb: h.matmul(
                            self.ps[pb][0:2, 0:512], lhsT=scc[:, kc, :], rhs=wm[a][:, kc, :],
                            start=(kc == 0), stop=(kc == KC - 1)),
                            reads=["wm%d" % a, "scc"], writes=["ps%d" % pb])
                    if ct % 2 == 0:
                        s.op("act", lambda h, ct=ct, pb=pb: h.activation(
                            out=modrow[:, ct * 512:(ct + 1) * 512], in_=self.ps[pb][0:2, 0:512], func=AF.Copy),
                            reads=["ps%d" % pb], writes=[("modrow", ct, ct + 1)])
                    else:
                        s.op("dve", lambda h, ct=ct, pb=pb: h.tensor_copy(
                            out=modrow[:, ct * 512:(ct + 1) * 512], in_=self.ps[pb][0:2, 0:512]),
                            reads=["ps%d" % pb], writes=[("modrow", ct, ct + 1)])
                for c in range(96):
                    s.op("pe", lambda h, c=c: h.transpose(self.ps[0][:, c * 2:c * 2 + 2],
                                                          modrow[:, c * 128:(c + 1) * 128], self.ident[0:2, 0:2]),
                         reads=[("modrow", c // 4, c // 4 + 1), "ident"], writes=["ps0"])
                s.op("dve", lambda h, l=l: h.tensor_tensor(
                    out=self.MOD[:, l, :, :], in0=self.ps[0][:, 0:192].rearrange("p (c s) -> p c s", s=2),
                    in1=bm[:, l, :, :], op=ALU.add),
                    reads=["ps0", "bm"], writes=["MOD"])
                s.op("dve", lambda h, l=l: h.scalar_tensor_tensor(
                    out=self.A1[:, l, :, :], in0=self.MOD[:, l, 16:32, :], scalar=1.0, in1=gm[:, l, :, :],
                    op0=ALU.add, op1=ALU.mult), reads=["MOD", "gm"], writes=["A1"])
                s.op("dve", lambda h, l=l: h.scalar_tensor_tensor(
                    out=self.A2[:, l, :, :], in0=self.MOD[:, l, 64:80, :], scalar=1.0, in1=gf[:, l, :, :],
                    op0=ALU.add, op1=ALU.mult), reads=["MOD", "gf"], writes=["A2"])
            s.barrier()

    def norm_mod(self, xt, xname, sq, h, hname, n, A, B, scr, scrname, psb=6):
        s = self.s
        half = KC // 2
        s.op("act", lambda e: e.activation(out=sq[:, 0:half, 0:n], in_=xt[:, 0:half, 0:n], func=AF.Square),
             reads=[xname], writes=[("sq", 0, half)])
        s.op("pool", lambda e: e.tensor_tensor(out=sq[:, half:KC, 0:n], in0=xt[:, half:KC, 0:n],
                                                in1=xt[:, half:KC, 0:n], op=ALU.mult),
             reads=[xname], writes=[("sq", half, KC)])
        pname = "ps%d" % psb
        for kc in range(KC):
            s.op("pe", lambda e, kc=kc: e.matmul(self.ps[psb][:, 0:n], lhsT=self.ones_b[:], rhs=sq[:, kc, 0:n],
                                                 start=(kc == 0), stop=(kc == KC - 1)),
                 reads=[("sq", kc, kc + 1), "ones_b"], writes=[pname])
        s.op("act", lambda e: e.activation(out=scr[:, 0:n], in_=self.ps[psb][:, 0:n], func=AF.Sqrt,
                                           scale=1.0 / D, bias=EPS),
             reads=[pname], writes=[scrname])
        s.op("dve", lambda e: e.reciprocal(out=scr[:, 0:n], in_=scr[:, 0:n]), reads=[scrname], writes=[scrname])
        for kc in range(KC):
            me = "dve" if kc % 2 == 0 else "pool"
            s.op(me, lambda e, kc=kc: e.tensor_tensor(out=xt[:, kc, 0:n], in0=xt[:, kc, 0:n], in1=scr[:, 0:n],
                                                      op=ALU.mult),
                 reads=[(xname, kc, kc + 1), scrname], writes=[(xname, kc, kc + 1)])
            if kc % 2 == 0:
                s.op("act", lambda e, kc=kc: e.activation(out=h[:, kc, 0:n], in_=xt[:, kc, 0:n], func=AF.Identity,
                                                          scale=A(kc), bias=B(kc)),
                     reads=[(xname, kc, kc + 1)], writes=[(hname, kc, kc + 1)])
            else:
                s.op("dve", lambda e, kc=kc: e.tensor_scalar(out=h[:, kc, 0:n], in0=xt[:, kc, 0:n], scalar1=A(kc),
                                                             scalar2=B(kc), op0=ALU.mult, op1=ALU.add),
                     reads=[(xname, kc, kc + 1)], writes=[(hname, kc, kc + 1)])

    def ffn(self, l, mid_hook=None):
        nc, s = self.nc, self.s
        L, T = self.L, self.T
        NW = 464
        tiles = seq_tiles(CT, 1, 0)
        for t in tiles:
            t["s"] = 1
        nlt = max(1, -(-L // 455))
        lt = seq_tiles(L, nlt, CT)
        for t in lt:
            t["s"] = 0
        tiles = (tiles if l != self.depth - 1 or not self.plan_is_full() else []) + lt
        XTv = self.XTv()
        wupv = self.wup_b[l].ap().rearrange("(j p) f -> j p f", p=128)
        wdnv = self.wdn_b[l].ap().rearrange("(c p) f -> c p f", p=128)
        with ExitStack() as st:
            xt = self.sb(st, "f_xt", [128, KC, NW], F32)
            sq = self.sb(st, "f_sq", [128, KC, NW], BF16)
            hb = self.sb(st, "f_h", [128, KC, NW], BF16)
            g = self.sb(st, "f_g", [128, NJ, NW], BF16)
            E = [self.sb(st, "f_E%d" % i, [128, 516], F32) for i in range(4)]
            acc = [self.sb(st, "f_acc%d" % i, [128, NW], F32) for i in range(4)]
            sa = [self.sb(st, "f_sa%d" % i, [128, NW], F32) for i in range(2)]
            xr = [self.sb(st, "f_xr%d" % i, [128, NW], F32) for i in range(3)]
            xo = [self.sb(st, "f_xo%d" % i, [128, NW], F32) for i in range(3)]
            scr = self.sb(st, "f_scr", [128, NW], F32)
            wup = [self.sb(st, "f_wup%d" % i, [128, KC, 256], BF16) for i in range(4)]
            wdn = [self.sb(st, "f_wdn%d" % i, [128, NJ, 128], BF16) for i in range(3)]
            H = self.sb(st, "f_H", [128, 88, 2], F32)
            cw = self.sb(st, "f_cw", [128, 88, 3], F32)
            cb = self.sb(st, "f_cb", [128, 88], F32)
            self.dma("sp", cw[:], self.cw_in.ap()[l], writes=["cw"])
            self.dma("sp", cb[:], self.cb_in.ap()[l], writes=["cb"])

            units = []
            for ti in range(len(tiles)):
                units += [("up", j) for j in range(NJ)] + [("dn", c) for c in range(16)]
            st_ = dict(nload=0, nup=0, ndn=0)
            slot_of = {}

            def ensure_loaded(upto):
                while st_["nload"] <= min(upto, len(units) - 1):
                    u = st_["nload"]
                    kind, idx = units[u]
                    if kind == "up":
                        a = st_["nup"] % 4
                        st_["nup"] += 1
                        self.dma("sp", wup[a][:], wupv[idx].rearrange("p (k f) -> p k f", k=KC),
                                 writes=["wup%d" % a], extra=self.bg[("wup", l)])
                    else:
                        a = st_["ndn"] % 3
                        st_["ndn"] += 1
                        self.dma("sp", wdn[a][:], wdnv[idx].rearrange("p (j f) -> p j f", j=NJ),
                                 writes=["wdn%d" % a], extra=self.bg[("wdn", l)])
                    slot_of[u] = a
                    st_["nload"] += 1

            def load_x(t):
                n = t["m_hi"] - t["m_lo"]
                self.dma("sp", xt[:, :, 0:n], XTv[:, :, t["m_lo"]:t["m_hi"]],
                         reads=[("XT", t["m_lo"], t["m_hi"])], writes=["f_xt"])

            def do_norm(t):
                n = t["m_hi"] - t["m_lo"]
                sidx = t["s"]
                self.norm_mod(xt, "f_xt", sq, hb, "f_h", n,
                              lambda kc: self.A2[:, l, kc, sidx:sidx + 1],
                              lambda kc: self.MOD[:, l, 48 + kc, sidx:sidx + 1], scr, "f_scr")

            bank = [0]
            ecnt = [0]
            ucur = [0]
            load_x(tiles[0])
            do_norm(tiles[0])
            def half_body(t, j, half, a, nm, no, p_off):
                f = j + NJ * half
                pb = bank[0] % 6
                bank[0] += 1
                for kc in range(KC):
                    s.op("pe", lambda e, kc=kc: e.matmul(
                        self.ps[pb][:, 0:nm], lhsT=wup[a][:, kc, half * 128:(half + 1) * 128],
                        rhs=hb[:, kc, 0:nm], start=(kc == 0), stop=(kc == KC - 1)),
                        reads=["wup%d" % a, ("f_h", kc, kc + 1)], writes=["ps%d" % pb])
                ei = ecnt[0] % 4
                ecnt[0] += 1
                Eb, ab = E[ei], acc[ei]
                en, an = "f_E%d" % ei, "f_acc%d" % ei
                if t["first"]:
                    s.op("pool", lambda e: e.memset(Eb[:, 0:1], 0.0), writes=[(en, 0, 1)])
                else:
                    s.op("pool", lambda e: e.tensor_copy(out=Eb[:, 0:2], in_=H[:, f, :]),
                         reads=[("H", f, f + 1)], writes=[(en, 0, 2)])
                s.op("act", lambda e: e.activation(
                    out=Eb[:, p_off:p_off + nm], in_=self.ps[pb][:, 0:nm], func=AF.Copy),
                    reads=["ps%d" % pb], writes=[(en, p_off, p_off + nm)])
                if t["last"]:
                    s.op("pool", lambda e: e.memset(Eb[:, no + 1:no + 2], 0.0),
                         writes=[(en, no + 1, no + 2)])
                else:
                    s.op("pool", lambda e: e.tensor_copy(
                        out=H[:, f, :], in_=Eb[:, p_off + nm - 2:p_off + nm]),
                        reads=[(en, p_off + nm - 2, p_off + nm)], writes=[("H", f, f + 1)])
                s.op("pool", lambda e: e.tensor_scalar(
                    out=ab[:, 0:no], in0=Eb[:, 0:no], scalar1=cw[:, f, 0:1], scalar2=cb[:, f:f + 1],
                    op0=ALU.mult, op1=ALU.add),
                    reads=[en, "cw", "cb"], writes=[an])
                s.op("dve", lambda e: e.scalar_tensor_tensor(
                    out=ab[:, 0:no], in0=Eb[:, 1:no + 1], scalar=cw[:, f, 1:2], in1=ab[:, 0:no],
                    op0=ALU.mult, op1=ALU.add), reads=[en, an, "cw"], writes=[an])
                s.op("dve", lambda e: e.scalar_tensor_tensor(
                    out=ab[:, 0:no], in0=Eb[:, 2:no + 2], scalar=cw[:, f, 2:3], in1=ab[:, 0:no],
                    op0=ALU.mult, op1=ALU.add), reads=[en, an, "cw"], writes=[an])
                return (ab, an)

            def pair_body(t, j, nm, no, p_off):
                u = ucur[0]
                ucur[0] += 1
                ensure_loaded(u + 2)
                a = slot_of[u]
                accs = [half_body(t, j, half, a, nm, no, p_off) for half in range(2)]
                si = j % 2
                s.op("act", lambda e: e.activation(out=sa[si][:, 0:no], in_=accs[0][0][:, 0:no], func=AF.Silu),
                     reads=[accs[0][1]], writes=["f_sa%d" % si])
                s.op("pool", lambda e: e.tensor_tensor(
                    out=g[:, j, 0:no], in0=sa[si][:, 0:no], in1=accs[1][0][:, 0:no], op=ALU.mult),
                    reads=["f_sa%d" % si, accs[1][1]], writes=[("f_g", j, j + 1)])

            def dn_body(t, c, no, sidx):
                u = ucur[0]
                ucur[0] += 1
                ensure_loaded(u + 2)
                a = slot_of[u]
                ri = c % 3
                self.dma("sp", xr[ri][:, 0:no], XTv[:, c, t["o_lo"]:t["o_hi"]],
                         reads=[("XT", t["o_lo"], t["o_hi"])], writes=["f_xr%d" % ri])
                pb = bank[0] % 6
                bank[0] += 1
                for j in range(NJ):
                    s.op("pe", lambda e, j=j: e.matmul(
                        self.ps[pb][:, 0:no], lhsT=wdn[a][:, j, :], rhs=g[:, j, 0:no],
                        start=(j == 0), stop=(j == NJ - 1)),
                        reads=["wdn%d" % a, ("f_g", j, j + 1)], writes=["ps%d" % pb])
                s.op("dve", lambda e: e.scalar_tensor_tensor(
                    out=xo[ri][:, 0:no], in0=self.ps[pb][:, 0:no], scalar=self.MOD[:, l, 80 + c, sidx:sidx + 1],
                    in1=xr[ri][:, 0:no], op0=ALU.mult, op1=ALU.add),
                    reads=["ps%d" % pb, "f_xr%d" % ri], writes=["f_xo%d" % ri])
                self.dma("sp", XTv[:, c, t["o_lo"]:t["o_hi"]], xo[ri][:, 0:no],
                         reads=["f_xo%d" % ri], writes=[("XT", t["o_lo"], t["o_hi"])])

            for ti, t in enumerate(tiles):
                nm = t["m_hi"] - t["m_lo"]
                no = t["o_hi"] - t["o_lo"]
                p_off = 1 if t["first"] else 2
                for j in range(NJ):
                    pair_body(t, j, nm, no, p_off)
                if mid_hook is not None and ti == min(2, len(tiles) - 1):
                    mid_hook()
                if ti + 1 < len(tiles):
                    load_x(tiles[ti + 1])
                    do_norm(tiles[ti + 1])
                for c in range(16):
                    dn_body(t, c, no, t["s"])
        s.barrier()

    def att(self, l, mid_hook=None):
        nc, s = self.nc, self.s
        L, T = self.L, self.T
        j = l // 2
        NB = T // 128
        XTv = self.XTv()
        wq_v = self.wqkv_b[j].ap().rearrange("(k p) f -> p k f", p=128)
        with ExitStack() as st:
            wq = self.sb(st, "a_wq", [128, KC, 3072], BF16)
            xt = self.sb(st, "a_xt", [128, KC, 512], F32)
            sq = self.sb(st, "a_sq", [128, KC, 512], BF16)
            hb = self.sb(st, "a_h", [128, KC, 512], BF16)
            scr = self.sb(st, "a_scr", [128, 512], F32)
            cs = self.sb(st, "a_cs", [128, 2, 512], F32)
            rp = self.sb(st, "a_rp", [128, 128], BF16)
            rpf = self.sb(st, "a_rpf", [128, 128], F32)
            qsb = [self.sb(st, "a_qsb%d" % i, [128, 512], BF16) for i in range(2)]
            t1 = [self.sb(st, "a_t1%d" % i, [128, 512], F32) for i in range(2)]
            t2 = [self.sb(st, "a_t2%d" % i, [128, 512], F32) for i in range(2)]
            qo = [self.sb(st, "a_qo%d" % i, [128, 512], BF16) for i in range(3)]
            vo = [self.sb(st, "a_vo%d" % i, [128, 512], BF16) for i in range(2)]
            for kc in range(KC):
                self.dma("sp", wq[:, kc, :], wq_v[:, kc, :], writes=[("a_wq", kc, kc + 1)], extra=self.bg[("wqkv", j)])
            self.dma("sp", rpf[:], self.rperm_in.ap(), writes=["a_rpf"])
            s.op("act", lambda e: e.activation(out=rp[:], in_=rpf[:], func=AF.Copy), reads=["a_rpf"], writes=["a_rp"])
            tiles = [(0, CT, 1)] + [(CT + i * 512, min(T, CT + (i + 1) * 512), 0) for i in range(-(-L // 512))]
            cnt = [0]

            def qk_mm(t0, n, hc):
                i = cnt[0]
                cnt[0] += 1
                pa, pb = (i % 3) * 2, (i % 3) * 2 + 1
                a2 = i % 2
                a3 = i % 3
                for kc in range(KC):
                    s.op("pe", lambda e, kc=kc: e.matmul(self.ps[pa][:, 0:n], lhsT=wq[:, kc, hc * 128:(hc + 1) * 128],
                                                         rhs=hb[:, kc, 0:n], start=(kc == 0), stop=(kc == KC - 1)),
                         reads=[("a_wq", kc, kc + 1), ("a_h", kc, kc + 1)], writes=["ps%d" % pa])
                s.op("act", lambda e: e.activation(out=qsb[a2][:, 0:n], in_=self.ps[pa][:, 0:n], func=AF.Copy),
                     reads=["ps%d" % pa], writes=["a_qsb%d" % a2])
                return i

            def qk_rope(t0, n, hc, i):
                pa, pb = (i % 3) * 2, (i % 3) * 2 + 1
                a2 = i % 2
                a3 = i % 3
                s.op("pe", lambda e: e.matmul(self.ps[pb][:, 0:n], lhsT=rp[:], rhs=qsb[a2][:, 0:n], start=True, stop=True),
                     reads=["a_rp", "a_qsb%d" % a2], writes=["ps%d" % pb])
                s.op("dve", lambda e: e.scalar_tensor_tensor(out=t1[a2][:, 0:n], in0=self.ps[pa][:, 0:n], scalar=1.0,
                                                             in1=cs[:, 0, 0:n], op0=ALU.mult, op1=ALU.mult),
                     reads=["ps%d" % pa, "a_cs", "a_qsb%d" % a2], writes=["a_t1%d" % a2])
                s.op("dve", lambda e: e.scalar_tensor_tensor(out=t2[a2][:, 0:n], in0=self.ps[pb][:, 0:n], scalar=1.0,
                                                             in1=cs[:, 1, 0:n], op0=ALU.mult, op1=ALU.mult),
                     reads=["ps%d" % pb, "a_cs"], writes=["a_t2%d" % a2])
                s.op("pool", lambda e: e.tensor_tensor(out=qo[a3][:, 0:n], in0=t1[a2][:, 0:n], in1=t2[a2][:, 0:n],
                                                       op=ALU.add),
                     reads=["a_t1%d" % a2, "a_t2%d" % a2], writes=["a_qo%d" % a3])
                if hc < 16:
                    dst = self.QT.ap()[:, hc, t0:t0 + n]
                    wr = ("QT", t0, t0 + n)
                else:
                    dst = self.KT.ap()[:, hc - 16, t0:t0 + n]
                    wr = ("KT", t0, t0 + n)
                self.dma("sp", dst, qo[a3][:, 0:n], reads=["a_qo%d" % a3], writes=[wr])

            def v_block(t0, blk):
                i = cnt[0]
                cnt[0] += 1
                pa = (i % 3) * 2
                a2 = i % 2
                for kc in range(KC):
                    s.op("pe", lambda e, kc=kc: e.matmul(self.ps[pa][:, 0:512], lhsT=hb[:, kc, blk * 128:(blk + 1) * 128],
                                                         rhs=wq[:, kc, 2560:3072], start=(kc == 0), stop=(kc == KC - 1)),
                         reads=[("a_wq", kc, kc + 1), ("a_h", kc, kc + 1)], writes=["ps%d" % pa])
                s.op("act", lambda e: e.activation(out=vo[a2][:], in_=self.ps[pa][:, 0:512], func=AF.Copy),
                     reads=["ps%d" % pa], writes=["a_vo%d" % a2])
                tb = t0 + blk * 128
                self.dma("sp", self.V.ap()[tb:tb + 128, :], vo[a2][:], reads=["a_vo%d" % a2], writes=[("V", tb, tb + 128)])

            def a1_tile(t0, t1_, sidx):
                n = t1_ - t0
                self.dma("sp", xt[:, :, 0:n], XTv[:, :, t0:t1_], reads=[("XT", t0, t1_)], writes=["a_xt"])
                self.dma("sp", cs[:, 0, 0:n], self.cos_in.ap()[:, t0:t1_], writes=[("a_cs", 0, 1)])
                self.dma("sp", cs[:, 1, 0:n], self.sin_in.ap()[:, t0:t1_], writes=[("a_cs", 1, 2)])
                self.norm_mod(xt, "a_xt", sq, hb, "a_h", n,
                              lambda kc: self.A1[:, l, kc, sidx:sidx + 1],
                              lambda kc: self.MOD[:, l, kc, sidx:sidx + 1], scr, "a_scr", psb=6)
                prev = None
                for hc in range(20):
                    i = qk_mm(t0, n, hc)
                    if prev is not None:
                        qk_rope(t0, n, prev[0], prev[1])
                    prev = (hc, i)
                v_block(t0, 0)
                qk_rope(t0, n, prev[0], prev[1])
                for blk in range(1, n // 128):
                    v_block(t0, blk)

            for (t0, t1_, sidx) in tiles:
                a1_tile(t0, t1_, sidx)
        s.barrier()
        if mid_hook is not None:
            mid_hook()
        with ExitStack() as st:
            kt = self.sb(st, "b_kt", [128, 4, T], BF16)
            vv = self.sb(st, "b_v", [128, NB, 512], BF16)
            wo = self.sb(st, "b_wo", [128, 16, D], BF16)
            mk = self.sb(st, "b_mk", [128, 2, 512], F32)
            esr = self.sb(st, "b_esr", [128, 16, 128], F32)
            qt = [self.sb(st, "b_qt%d" % i, [128, 16, 128], BF16) for i in range(2)]
            P = [self.sb(st, "b_P%d" % i, [128, 512], BF16) for i in range(10)]
            tm = [self.sb(st, "b_tm%d" % i, [128, 512], F32) for i in range(2)]
            rd = [self.sb(st, "b_rd%d" % i, [128, 512], F32) for i in range(2)]
            ot = [self.sb(st, "b_ot%d" % i, [128, 16, 128], BF16) for i in range(2)]
            xr1 = self.sb(st, "b_xr0", [128, KC, 128], F32)
            xo1 = self.sb(st, "b_xo0", [128, KC, 128], F32)
            xr = [xr1, xr1]
            xo = [xo1, xo1]
            for g in range(4):
                self.dma("sp", kt[:, g, :], self.KT.ap()[:, g, :], reads=["KT"], writes=["b_kt"])
            self.dma("sp", vv[:], self.V.ap().rearrange("(b p) f -> p b f", p=128), reads=["V"], writes=["b_v"])
            wo_v = self.wo_b[j].ap().rearrange("(h p) f -> p h f", p=128)
            for hd in range(16):
                self.dma("sp", wo[:, hd, :], wo_v[:, hd, :], writes=["b_wo"], extra=self.bg[("wo", j)])
            self.dma("sp", mk[:], self.mask_in.ap(), writes=["b_mk"])
            self.dma("sp", esr[:], self.sink_in.ap()[j], writes=["b_esr"])
            s.op("act", lambda e: e.activation(out=esr[:], in_=esr[:], func=AF.Exp), reads=["b_esr"], writes=["b_esr"])
            nlat = L // 128
            sc_ = 128 ** -0.5
            pcnt = [0]
            scnt = [0]
            mcnt = [0]

            def group(bq, gq, kbs, a):
                Ps = []
                for (kb, mi) in kbs:
                    pi = pcnt[0] % 10
                    pcnt[0] += 1
                    sb_ = scnt[0] % 3
                    scnt[0] += 1
                    s.op("pe", lambda e, kb=kb, sb_=sb_: e.matmul(
                        self.ps[sb_][:, :], lhsT=kt[:, gq, kb * 128:(kb + 1) * 128],
                        rhs=qt[a][:, gq * 4:(gq + 1) * 4, :].rearrange("p h q -> p (h q)"), start=True, stop=True),
                        reads=["b_kt", "b_qt%d" % a], writes=["ps%d" % sb_])
                    if mi is None:
                        s.op("act", lambda e, pi=pi, sb_=sb_: e.activation(out=P[pi][:], in_=self.ps[sb_][:, :],
                                                                           func=AF.Exp, scale=sc_),
                             reads=["ps%d" % sb_], writes=["b_P%d" % pi])
                    else:
                        ti = mcnt[0] % 2
                        mcnt[0] += 1
                        s.op("dve", lambda e, ti=ti, sb_=sb_, mi=mi: e.scalar_tensor_tensor(
                            out=tm[ti][:], in0=self.ps[sb_][:, :], scalar=1.0, in1=mk[:, mi, :], op0=ALU.mult,
                            op1=ALU.add),
                            reads=["ps%d" % sb_, "b_mk"], writes=["b_tm%d" % ti])
                        s.op("act", lambda e, pi=pi, ti=ti: e.activation(out=P[pi][:], in_=tm[ti][:], func=AF.Exp,
                                                                         scale=sc_),
                             reads=["b_tm%d" % ti], writes=["b_P%d" % pi])
                    Ps.append((kb, pi))
                nk = len(Ps)
                for i, (kb, pi) in enumerate(Ps):
                    s.op("pe", lambda e, pi=pi, i=i: e.matmul(self.ps[3][:, :], lhsT=self.ones_b[:], rhs=P[pi][:],
                                                              start=(i == 0), stop=(i == nk - 1)),
                         reads=["b_P%d" % pi, "ones_b"], writes=["ps3"])
                for hh in range(4):
                    for i, (kb, pi) in enumerate(Ps):
                        s.op("pe", lambda e, pi=pi, i=i, kb=kb, hh=hh: e.matmul(
                            self.ps[4][:, hh * 128:(hh + 1) * 128], lhsT=vv[:, kb, gq * 128:(gq + 1) * 128],
                            rhs=P[pi][:, hh * 128:(hh + 1) * 128], start=(i == 0), stop=(i == nk - 1)),
                            reads=["b_P%d" % pi, "b_v"], writes=["ps4"])
                ri = gq % 2
                s.op("dve", lambda e: e.scalar_tensor_tensor(
                    out=rd[ri][:], in0=self.ps[3][:, :], scalar=1.0,
                    in1=esr[:, gq * 4:(gq + 1) * 4, :].rearrange("p h q -> p (h q)"), op0=ALU.mult, op1=ALU.add),
                    reads=["ps3", "b_esr"], writes=["b_rd%d" % ri])
                s.op("act", lambda e: e.activation(out=rd[ri][:], in_=rd[ri][:], func=AF.Ln), reads=["b_rd%d" % ri],
                     writes=["b_rd%d" % ri])
                s.op("act", lambda e: e.activation(out=rd[ri][:], in_=rd[ri][:], func=AF.Exp, scale=-1.0),
                     reads=["b_rd%d" % ri], writes=["b_rd%d" % ri])
                s.op("dve", lambda e: e.scalar_tensor_tensor(
                    out=ot[a][:, gq * 4:(gq + 1) * 4, :].rearrange("p h q -> p (h q)"), in0=self.ps[4][:, :],
                    scalar=1.0, in1=rd[ri][:], op0=ALU.mult, op1=ALU.mult),
                    reads=["ps4", "b_rd%d" % ri], writes=[("b_ot%d" % a, gq, gq + 1)])

            def qblock(bq):
                a = bq % 2
                t0 = bq * 128
                sidx = 1 if bq < 2 else 0
                self.dma("sp", qt[a][:], self.QT.ap()[:, :, t0:t0 + 128], reads=[("QT", t0, t0 + 128)],
                         writes=["b_qt%d" % a])
                kbs = [(0, None), (1, None)]
                if bq >= 2:
                    n = bq - 2
                    if n > 0:
                        kbs.append((bq - 1, 0))
                    kbs.append((bq, None))
                    if n < nlat - 1:
                        kbs.append((bq + 1, 1))
                for gq in range(4):
                    group(bq, gq, kbs, a)
                    if bq > 0:
                        qblock_back(bq - 1, gq)

            def qblock_back(bq, q4):
                a = bq % 2
                t0 = bq * 128
                sidx = 1 if bq < 2 else 0
                if q4 == 0:
                    self.dma("sp", xr[a][:], XTv[:, :, t0:t0 + 128], reads=[("XT", t0, t0 + 128)], writes=["b_xr0"])
                if True:
                    pb = 5 + (bq * 4 + q4) % 3
                    for r in range(4):
                        dc = q4 * 4 + r
                        for hd in range(16):
                            s.op("pe", lambda e, dc=dc, hd=hd, r=r, pb=pb: e.matmul(
                                self.ps[pb][:, r * 128:(r + 1) * 128], lhsT=wo[:, hd, dc * 128:(dc + 1) * 128],
                                rhs=ot[a][:, hd, :], start=(hd == 0), stop=(hd == 15)),
                                reads=["b_wo", "b_ot%d" % a], writes=["ps%d" % pb])
                    for r in range(4):
                        dc = q4 * 4 + r
                        s.op("dve", lambda e, dc=dc, r=r, pb=pb: e.scalar_tensor_tensor(
                            out=xo[a][:, dc, :], in0=self.ps[pb][:, r * 128:(r + 1) * 128],
                            scalar=self.MOD[:, l, 32 + dc, sidx:sidx + 1], in1=xr[a][:, dc, :],
                            op0=ALU.mult, op1=ALU.add),
                            reads=["ps%d" % pb, "b_xr0"], writes=[("b_xo0", dc, dc + 1)])
                if q4 == 3:
                    self.dma("sp", XTv[:, :, t0:t0 + 128], xo[a][:], reads=["b_xo0"], writes=[("XT", t0, t0 + 128)])

            for bq in range(NB):
                qblock(bq)
            for q4 in range(4):
                qblock_back(NB - 1, q4)
        s.barrier()

    def mls(self, l, mid_hook=None):
        nc, s = self.nc, self.s
        L, T = self.L, self.T
        j = l // 2
        NB = T // 128
        CHS = 128
        NCH = T // CHS
        NCTX = CT // CHS
        XTv = self.XTv()
        win_v = self.win_b[j].ap().rearrange("(k p) f -> p k f", p=128)
        KS = 128 ** -0.5
        with ExitStack() as st:
            xt = self.sb(st, "m_xt", [128, KC, 512], F32)
            sq = self.sb(st, "m_sq", [128, KC, 512], BF16)
            hb = self.sb(st, "m_h", [128, KC, 512], BF16)
            scr = self.sb(st, "m_scr", [128, 512], F32)
            wt = [self.sb(st, "m_wt%d" % i, [128, KC, 512], BF16) for i in range(3)]
            brow = self.sb(st, "m_brow", [128, 5152], F32)
            bqk = self.sb(st, "m_bqk", [128, 16], F32)
            oa = [self.sb(st, "m_oa%d" % i, [128, 512], BF16) for i in range(3)]
            ob = [self.sb(st, "m_ob%d" % i, [128, 512], BF16) for i in range(3)]
            of = [self.sb(st, "m_of%d" % i, [128, 512], F32) for i in range(2)]
            og = [self.sb(st, "m_og%d" % i, [128, 32], F32) for i in range(2)]
            self.dma("sp", brow[:], self.mbrow_in.ap()[j], writes=["m_brow"])
            self.dma("sp", bqk[:], self.mbqk_in.ap()[j], writes=["m_bqk"])
            s.op("dve", lambda e: e.tensor_scalar(out=bqk[:, 8:16], in0=bqk[:, 8:16], scalar1=KS, scalar2=None,
                                                  op0=ALU.mult), reads=["m_bqk"], writes=["m_bqk"])
            s.op("dve", lambda e: e.tensor_scalar(out=brow[:, 0:1024], in0=brow[:, 0:1024], scalar1=KS, scalar2=None,
                                                  op0=ALU.mult), reads=["m_brow"], writes=["m_brow"])
            tiles = [(0, CT, 1)] + [(CT + i * 512, min(T, CT + (i + 1) * 512), 0) for i in range(-(-L // 512))]
            units = [(0, 512, "q"), (512, 512, "q"), (1024, 512, "k"), (1536, 512, "k")] + \
                    [(2048 + i * 512, 512, "v") for i in range(4)] + [(4096 + i * 512, 512, "o") for i in range(4)] + \
                    [(6144, 32, "g")]
            cnt = dict(w=0, p=0, a=0, b=0, f=0, g=0)

            def load_unit(u):
                col0, ncol, kind = u
                a = cnt["w"] % 3
                cnt["w"] += 1
                self.dma("sp", wt[a][:, :, 0:ncol], win_v[:, :, col0:col0 + ncol], writes=["m_wt%d" % a],
                         extra=self.bg[("win", j)])
                return a

            def feat_major(t0, n, a, u, c):
                col0, ncol, kind = u
                ch = (col0 + c * 128) // 128
                pb = cnt["p"] % 6
                cnt["p"] += 1
                for kc in range(KC):
                    s.op("pe", lambda e, kc=kc: e.matmul(self.ps[pb][:, 0:n], lhsT=wt[a][:, kc, c * 128:(c + 1) * 128],
                                                         rhs=hb[:, kc, 0:n], start=(kc == 0), stop=(kc == KC - 1)),
                         reads=["m_wt%d" % a, ("m_h", kc, kc + 1)], writes=["ps%d" % pb])
                oi = cnt["a"] % 3
                cnt["a"] += 1
                sc = KS if kind == "k" else 1.0
                s.op("act", lambda e: e.activation(out=oa[oi][:, 0:n], in_=self.ps[pb][:, 0:n], func=AF.Identity,
                                                   scale=sc, bias=bqk[:, ch:ch + 1]),
                     reads=["ps%d" % pb, "m_bqk"], writes=["m_oa%d" % oi])
                dst = (self.MQT if kind == "q" else self.MKT).ap()[:, ch % 8, t0:t0 + n]
                self.dma("sp", dst, oa[oi][:, 0:n], reads=["m_oa%d" % oi],
                         writes=[("MQT" if kind == "q" else "MKT", t0, t0 + n)])

            def tok_major(t0, blk, a, u):
                col0, ncol, kind = u
                pb = cnt["p"] % 6
                cnt["p"] += 1
                tb = t0 + blk * 128
                for kc in range(KC):
                    s.op("pe", lambda e, kc=kc: e.matmul(self.ps[pb][:, 0:ncol], lhsT=hb[:, kc, blk * 128:(blk + 1) * 128],
                                                         rhs=wt[a][:, kc, 0:ncol], start=(kc == 0), stop=(kc == KC - 1)),
                         reads=["m_wt%d" % a, ("m_h", kc, kc + 1)], writes=["ps%d" % pb])
                bcol = col0 - 1024
                if kind in ("k", "v"):
                    oi = cnt["b"] % 3
                    cnt["b"] += 1
                    sc = KS if kind == "k" else 1.0
                    s.op("dve", lambda e: e.scalar_tensor_tensor(
                        out=ob[oi][:, :], in0=self.ps[pb][:, 0:512], scalar=sc, in1=brow[:, bcol:bcol + 512],
                        op0=ALU.mult, op1=ALU.add), reads=["ps%d" % pb, "m_brow"], writes=["m_ob%d" % oi])
                    if kind == "k":
                        dst = self.MK.ap()[tb:tb + 128, col0 - 1024:col0 - 1024 + 512]
                        wr = ("MK", tb, tb + 128)
                    else:
                        dst = self.MV.ap()[tb:tb + 128, col0 - 2048:col0 - 2048 + 512]
                        wr = ("MV", tb, tb + 128)
                    self.dma("sp", dst, ob[oi][:, :], reads=["m_ob%d" % oi], writes=[wr])
                elif kind == "o":
                    fi = cnt["f"] % 2
                    cnt["f"] += 1
                    oi = cnt["b"] % 3
                    cnt["b"] += 1
                    s.op("dve", lambda e: e.scalar_tensor_tensor(
                        out=of[fi][:, :], in0=self.ps[pb][:, 0:512], scalar=1.0, in1=brow[:, bcol:bcol + 512],
                        op0=ALU.mult, op1=ALU.add), reads=["ps%d" % pb, "m_brow"], writes=["m_of%d" % fi])
                    s.op("act", lambda e: e.activation(out=ob[oi][:, :], in_=of[fi][:, :], func=AF.Sigmoid),
                         reads=["m_of%d" % fi], writes=["m_ob%d" % oi])
                    self.dma("sp", self.SIGO.ap()[tb:tb + 128, col0 - 4096:col0 - 4096 + 512], ob[oi][:, :],
                             reads=["m_ob%d" % oi], writes=[("SIGO", tb, tb + 128)])
                else:
                    gi = cnt["g"] % 2
                    cnt["g"] += 1
                    s.op("dve", lambda e: e.scalar_tensor_tensor(
                        out=og[gi][:, :], in0=self.ps[pb][:, 0:32], scalar=1.0, in1=brow[:, bcol:bcol + 32],
                        op0=ALU.mult, op1=ALU.add), reads=["ps%d" % pb, "m_brow"], writes=["m_og%d" % gi])
                    self.dma("sp", self.GATES.ap()[tb:tb + 128, :], og[gi][:, :], reads=["m_og%d" % gi],
                             writes=[("GATES", tb, tb + 128)])

            def m1_tile(t0, t1_, sidx):
                n = t1_ - t0
                self.dma("sp", xt[:, :, 0:n], XTv[:, :, t0:t1_], reads=[("XT", t0, t1_)], writes=["m_xt"])
                self.norm_mod(xt, "m_xt", sq, hb, "m_h", n,
                              lambda kc: self.A1[:, l, kc, sidx:sidx + 1],
                              lambda kc: self.MOD[:, l, kc, sidx:sidx + 1], scr, "m_scr", psb=6)
                nxt = load_unit(units[0])
                for ui, u in enumerate(units):
                    a = nxt
                    if ui + 1 < len(units):
                        nxt = load_unit(units[ui + 1])
                    if u[2] in ("q", "k"):
                        for c in range(4):
                            feat_major(t0, n, a, u, c)
                    if u[2] != "q":
                        for blk in range(n // 128):
                            tok_major(t0, blk, a, u)

            for (t0, t1_, sidx) in tiles:
                m1_tile(t0, t1_, sidx)
        s.barrier()
        if mid_hook is not None:
            mid_hook()
        with ExitStack() as st:
            A = [self.sb(st, "g_A%d" % d, [CHS, NCH, 8], F32) for d in range(2)]
            U = [self.sb(st, "g_U%d" % d, [CHS, NCH, 8], F32) for d in range(2)]
            AE = [self.sb(st, "g_AE%d" % d, [128, NCH, 8], F32) for d in range(2)]
            tri = self.sb(st, "g_tri", [CHS, 2, CHS], F32)
            onesf = self.sb(st, "g_ones", [CHS, 128], F32)
            with ExitStack() as st2:
                G = self.sb(st2, "g_G", [CHS, NCH, 32], F32)
                LF = self.sb(st2, "g_LF", [CHS, 2, NCH, 8], F32)
                tmp = self.sb(st2, "g_tmp", [CHS, 512], F32)
                self.dma("sp", G[:], self.GATES.ap().rearrange("(c p) g -> p c g", p=CHS), reads=["GATES"], writes=["g_G"])
                self.dma("sp", tri[:], self.tri_in.ap(), writes=["g_tri"])
                s.op("pool", lambda e: e.memset(onesf[:], 1.0), writes=["g_ones"])
                for d in range(2):
                    fc = 8 + 16 * d
                    s.op("act", lambda e, d=d, fc=fc: e.activation(out=LF[:, d, :, :], in_=G[:, :, fc:fc + 8],
                                                                   func=AF.Exp, scale=-1.0),
                         reads=["g_G"], writes=[("g_LF", d, d + 1)])
                    s.op("act", lambda e, d=d: e.activation(out=LF[:, d, :, :], in_=LF[:, d, :, :], func=AF.Ln,
                                                            scale=1.0, bias=1.0),
                         reads=[("g_LF", d, d + 1)], writes=[("g_LF", d, d + 1)])
                    s.op("dve", lambda e, d=d: e.tensor_scalar(out=LF[:, d, :, :], in0=LF[:, d, :, :], scalar1=-1.0,
                                                               scalar2=None, op0=ALU.mult),
                         reads=[("g_LF", d, d + 1)], writes=[("g_LF", d, d + 1)])
                CH = 60
                for d in range(2):
                    ic = 16 * d
                    for c0 in range(0, NCH, CH):
                        c1 = min(NCH, c0 + CH)
                        ncol = (c1 - c0) * 8
                        rhs = LF[:, d, c0:c1, :].rearrange("p c h -> p (c h)")
                        s.op("pe", lambda e, d=d, rhs=rhs, ncol=ncol: e.matmul(self.ps[0][0:CHS, 0:ncol], lhsT=tri[:, d, :],
                                                                              rhs=rhs, start=True, stop=True),
                             reads=["g_LF", "g_tri"], writes=["ps0"])
                        s.op("pe", lambda e, rhs=rhs, ncol=ncol: e.matmul(self.ps[1][:, 0:ncol], lhsT=onesf[:], rhs=rhs,
                                                                          start=True, stop=True),
                             reads=["g_LF", "g_ones"], writes=["ps1"])
                        s.op("act", lambda e, d=d, c0=c0, c1=c1, ncol=ncol: e.activation(
                            out=A[d][:, c0:c1, :].rearrange("p c h -> p (c h)"), in_=self.ps[0][0:CHS, 0:ncol], func=AF.Exp),
                            reads=["ps0"], writes=["g_A%d" % d])
                        s.op("dve", lambda e, d=d, c0=c0, c1=c1, ncol=ncol, ic=ic: e.scalar_tensor_tensor(
                            out=tmp[:, 0:ncol].rearrange("p (c h) -> p c h", h=8),
                            in0=self.ps[0][0:CHS, 0:ncol].rearrange("p (c h) -> p c h", h=8), scalar=-1.0,
                            in1=G[:, c0:c1, ic:ic + 8], op0=ALU.mult, op1=ALU.add),
                            reads=["ps0", "g_G"], writes=["g_tmp"])
                        s.op("act", lambda e, d=d, c0=c0, c1=c1, ncol=ncol: e.activation(
                            out=U[d][:, c0:c1, :].rearrange("p c h -> p (c h)"), in_=tmp[:, 0:ncol], func=AF.Exp),
                            reads=["g_tmp"], writes=["g_U%d" % d])
                        s.op("act", lambda e, d=d, c0=c0, c1=c1, ncol=ncol: e.activation(
                            out=AE[d][:, c0:c1, :].rearrange("p c h -> p (c h)"), in_=self.ps[1][:, 0:ncol], func=AF.Exp),
                            reads=["ps1"], writes=["g_AE%d" % d])
            s.barrier()
            GS = 256 // CHS
            qT = [[self.sb(st, "s_qT%d%d" % (d, i), [128, 8, GS * CHS], BF16) for i in range(2)] for d in range(2)]
            kT = [[self.sb(st, "s_kT%d%d" % (d, i), [128, 8, GS * CHS], BF16) for i in range(2)] for d in range(2)]
            kk = [[self.sb(st, "s_kk%d%d" % (d, i), [CHS, 8, 128], BF16) for i in range(2)] for d in range(2)]
            ku = [[self.sb(st, "s_ku%d%d" % (d, i), [CHS, 8, 128], BF16) for i in range(2)] for d in range(2)]
            va = [[self.sb(st, "s_va%d%d" % (d, i), [CHS, 8, 257], BF16) for i in range(2)] for d in range(2)]
            stm = [[self.sb(st, "s_stm%d%d" % (d, i), [CHS, 8, CHS], BF16) for i in range(2)] for d in range(2)]
            hbuf = [[self.sb(st, "s_hb%d%d" % (d, i), [CHS, 8, 256], F32) for i in range(2)] for d in range(2)]
            Dst = [self.sb(st, "s_D%d" % d, [128, 8, 257], F32) for d in range(2)]
            Cb = [[self.sb(st, "s_Cb%d%d" % (d, i), [128, 8, 257], BF16) for i in range(2)] for d in range(2)]
            r8 = [self.sb(st, "s_r8%d" % d, [CHS, 8], F32) for d in range(2)]
            r8n = [self.sb(st, "s_r8n%d" % d, [CHS, 8], F32) for d in range(2)]
            for d in range(2):
                s.op("pool", lambda e, d=d: e.memset(Dst[d][:], 0.0), writes=["s_D%d" % d])
                s.op("pool", lambda e, d=d: e.memset(Cb[d][0][:], 0.0), writes=["s_Cb%d0" % d])
                for i in range(2):
                    s.op("pool", lambda e, d=d, i=i: e.memset(va[d][i][:, :, 256:257], 1.0),
                         writes=[("s_va%d%d" % (d, i), 256, 257)])
            order = [list(range(NCH)), list(range(NCTX - 1, -1, -1)) + list(range(NCH - 1, NCTX - 1, -1))]
            gorder = []
            for d in range(2):
                go = []
                for c in order[d]:
                    if not go or go[-1] != c // GS:
                        go.append(c // GS)
                gorder.append(go)
            gstate = [dict(pos=-1, loaded=0), dict(pos=-1, loaded=0)]
            HO = [self.HF, self.HB]

            def load_group(d, gi):
                g = gorder[d][gi]
                a = gi % 2
                self.dma("sp", qT[d][a][:], self.MQT.ap()[:, :, g * GS * CHS:(g + 1) * GS * CHS], reads=["MQT"],
                         writes=["s_qT%d%d" % (d, a)])
                self.dma("sp", kT[d][a][:], self.MKT.ap()[:, :, g * GS * CHS:(g + 1) * GS * CHS], reads=["MKT"],
                         writes=["s_kT%d%d" % (d, a)])

            def scan_step(d, si):
                c = order[d][si]
                prev_c = order[d][si - 1] if si > 0 else None
                gs_ = gstate[d]
                if gs_["pos"] < 0 or gorder[d][gs_["pos"]] != c // GS:
                    gs_["pos"] += 1
                    while gs_["loaded"] <= min(gs_["pos"] + 1, len(gorder[d]) - 1):
                        load_group(d, gs_["loaded"])
                        gs_["loaded"] += 1
                ga = gs_["pos"] % 2
                co = (c % GS) * CHS
                a = si % 2
                nm = "%d%d" % (d, a)
                t0 = c * CHS
                self.dma("sp", kk[d][a][:], self.MK.ap()[t0:t0 + CHS, :].rearrange("p (h e) -> p h e", h=8), reads=["MK"],
                         writes=["s_kk" + nm])
                self.dma("sp", va[d][a][:, :, 0:256], self.MV.ap()[t0:t0 + CHS, :].rearrange("p (h e) -> p h e", h=8),
                         reads=["MV"], writes=[("s_va" + nm, 0, 256)])
                for hh in range(2):
                    for h4 in range(4):
                        h = hh * 4 + h4
                        s.op("pe", lambda e, h=h, h4=h4: e.matmul(self.ps[4][0:CHS, h4 * CHS:(h4 + 1) * CHS],
                                                                 lhsT=kT[d][ga][:, h, co:co + CHS],
                                                                 rhs=qT[d][ga][:, h, co:co + CHS], start=True, stop=True),
                             reads=["s_kT%d%d" % (d, ga), "s_qT%d%d" % (d, ga)], writes=["ps4"])
                    for h4 in range(4):
                        h = hh * 4 + h4
                        s.op("dve", lambda e, h=h, h4=h4: e.scalar_tensor_tensor(
                            out=stm[d][a][:, h, :], in0=self.ps[4][0:CHS, h4 * CHS:(h4 + 1) * CHS],
                            scalar=U[d][:, c, h:h + 1], in1=tri[:, d, :], op0=ALU.mult, op1=ALU.mult),
                            reads=["ps4", "g_tri"], writes=[("s_stm" + nm, h, h + 1)])
                for h in range(8):
                    s.op("pool", lambda e, h=h: e.tensor_scalar(
                        out=ku[d][a][:, h, :], in0=kk[d][a][:, h, :], scalar1=U[d][:, c, h:h + 1], scalar2=1.0,
                        op0=ALU.mult, op1=ALU.mult),
                        reads=["s_kk" + nm], writes=[("s_ku" + nm, h, h + 1)])
                if si + 1 < len(order[d]):
                    for h in range(8):
                        pb = 6 + (h % 2)
                        s.op("pe", lambda e, h=h, pb=pb: e.matmul(
                            self.ps[pb][:, 0:257], lhsT=ku[d][a][:, h, :], rhs=va[d][a][:, h, :], start=True, stop=True),
                            reads=[("s_ku" + nm, h, h + 1), "s_va" + nm], writes=["ps%d" % pb])
                        if prev_c is None:
                            s.op("dve", lambda e, h=h, pb=pb: e.tensor_copy(out=Dst[d][:, h, :], in_=self.ps[pb][:, 0:257]),
                                 reads=["ps%d" % pb], writes=[("s_D%d" % d, h, h + 1)])
                        else:
                            s.op("dve", lambda e, h=h, pb=pb: e.scalar_tensor_tensor(
                                out=Dst[d][:, h, :], in0=Dst[d][:, h, :], scalar=AE[d][:, prev_c, h:h + 1],
                                in1=self.ps[pb][:, 0:257], op0=ALU.mult, op1=ALU.add),
                                reads=["ps%d" % pb, ("s_D%d" % d, h, h + 1)], writes=[("s_D%d" % d, h, h + 1)])
                        s.op("pool", lambda e, h=h: e.tensor_scalar(
                            out=Cb[d][(si + 1) % 2][:, h, :], in0=Dst[d][:, h, :], scalar1=AE[d][:, c, h:h + 1],
                            scalar2=1.0, op0=ALU.mult, op1=ALU.mult),
                             reads=[("s_D%d" % d, h, h + 1)], writes=[("s_Cb%d%d" % (d, (si + 1) % 2), h, h + 1)])

                for h in range(8):
                    pb = h // 2
                    o_ = (h % 2) * 256
                    s.op("pe", lambda e, h=h, pb=pb, o_=o_: e.matmul(
                        self.ps[pb][0:CHS, o_:o_ + 256], lhsT=qT[d][ga][:, h, co:co + CHS], rhs=Cb[d][si % 2][:, h, 0:256],
                        start=True, stop=False),
                        reads=["s_qT%d%d" % (d, ga), ("s_Cb%d%d" % (d, si % 2), h, h + 1)], writes=["ps%d" % pb])
                    s.op("pe", lambda e, h=h, pb=pb, o_=o_: e.matmul(
                        self.ps[pb][0:CHS, o_:o_ + 256], lhsT=stm[d][a][:, h, :], rhs=va[d][a][:, h, 0:256],
                        start=False, stop=True),
                        reads=[("s_stm" + nm, h, h + 1), "s_va" + nm], writes=["ps%d" % pb])
                for h in range(8):
                    s.op("pe", lambda e, h=h: e.matmul(
                        self.ps[5][0:CHS, h:h + 1], lhsT=qT[d][ga][:, h, co:co + CHS], rhs=Cb[d][si % 2][:, h, 256:257],
                        start=True, stop=False),
                        reads=["s_qT%d%d" % (d, ga), ("s_Cb%d%d" % (d, si % 2), h, h + 1)], writes=["ps5"])
                    s.op("pe", lambda e, h=h: e.matmul(
                        self.ps[5][0:CHS, h:h + 1], lhsT=stm[d][a][:, h, :], rhs=va[d][a][:, h, 256:257],
                        start=False, stop=True),
                        reads=[("s_stm" + nm, h, h + 1), "s_va" + nm], writes=["ps5"])
                rr = r8[d]
                rn = "s_r8%d" % d
                s.op("dve", lambda e: e.scalar_tensor_tensor(out=rr[:], in0=self.ps[5][0:CHS, 0:8], scalar=1.0,
                                                             in1=A[d][:, c, :], op0=ALU.mult, op1=ALU.mult),
                     reads=["ps5"], writes=[rn])
                rneg = r8n[d]
                s.op("dve", lambda e: e.tensor_scalar(out=rneg[:], in0=rr[:], scalar1=-1.0, scalar2=None, op0=ALU.mult),
                     reads=[rn], writes=[rn + "n"])
                s.op("dve", lambda e: e.scalar_tensor_tensor(out=rr[:], in0=rr[:], scalar=1.0, in1=rneg[:],
                                                             op0=ALU.max, op1=ALU.max),
                     reads=[rn, rn + "n"], writes=[rn])
                s.op("dve", lambda e: e.reciprocal(out=rr[:], in_=rr[:]), reads=[rn], writes=[rn])
                s.op("dve", lambda e: e.tensor_tensor(out=rr[:], in0=rr[:], in1=A[d][:, c, :], op=ALU.mult),
                     reads=[rn], writes=[rn])
                for h in range(8):
                    pb = h // 2
                    o_ = (h % 2) * 256
                    s.op("act", lambda e, h=h, pb=pb, o_=o_: e.activation(
                        out=hbuf[d][a][:, h, :], in_=self.ps[pb][0:CHS, o_:o_ + 256], func=AF.Copy, scale=rr[:, h:h + 1]),
                        reads=["ps%d" % pb, rn], writes=[("s_hb" + nm, h, h + 1)])
                self.dma("sp", HO[d].ap()[t0:t0 + CHS, :], hbuf[d][a][:].rearrange("p h e -> p (h e)"),
                         reads=["s_hb" + nm], writes=[("H%d" % d, t0, t0 + CHS)])
            for si in range(NCH):
                for d in range(2):
                    scan_step(d, si)
        s.barrier()
        with ExitStack() as st:
            mwo = self.sb(st, "o_wo", [128, KC, D], BF16)
            gh = self.sb(st, "o_gh", [128, D], F32)
            idb = self.sb(st, "o_idb", [128, 128], BF16)
            hf = [self.sb(st, "o_hf%d" % i, [128, D], F32) for i in range(2)]
            hbk = [self.sb(st, "o_hbk%d" % i, [128, D], F32) for i in range(2)]
            so = [self.sb(st, "o_so%d" % i, [128, D], BF16) for i in range(2)]
            sqh = self.sb(st, "o_sq", [128, D], F32)
            ms = [self.sb(st, "o_ms%d" % i, [128, 8], F32) for i in range(2)]
            yb = [self.sb(st, "o_yb%d" % i, [128, D], BF16) for i in range(2)]
            yT = [self.sb(st, "o_yT%d" % i, [128, KC, 128], BF16) for i in range(2)]
            xr = [self.sb(st, "o_xr%d" % i, [128, KC, 128], F32) for i in range(2)]
            xo = [self.sb(st, "o_xo%d" % i, [128, KC, 128], F32) for i in range(2)]
            wo_v = self.mwo_b[j].ap().rearrange("(k p) f -> p k f", p=128)
            for kc in range(KC):
                self.dma("sp", mwo[:, kc, :], wo_v[:, kc, :], writes=["o_wo"], extra=self.bg[("mwo", j)])
            self.dma("sp", gh[:], self.ghead_in.ap()[j], writes=["o_gh"])
            s.op("act", lambda e: e.activation(out=idb[:], in_=self.ident[:], func=AF.Copy), reads=["ident"],
                 writes=["o_idb"])
            psb16 = [p.bitcast(BF16) for p in self.ps]

            def m4_block(b):
                a = b % 2
                t0 = b * 128
                sidx = 1 if t0 < CT else 0
                n_ = "%d" % a
                self.dma("sp", hf[a][:], self.HF.ap()[t0:t0 + 128, :], reads=[("H0", t0, t0 + 128)], writes=["o_hf" + n_])
                self.dma("sp", hbk[a][:], self.HB.ap()[t0:t0 + 128, :], reads=[("H1", t0, t0 + 128)], writes=["o_hbk" + n_])
                self.dma("sp", so[a][:], self.SIGO.ap()[t0:t0 + 128, :], reads=[("SIGO", t0, t0 + 128)], writes=["o_so" + n_])
                s.op("dve", lambda e: e.tensor_tensor(out=hf[a][:], in0=hf[a][:], in1=hbk[a][:], op=ALU.add),
                     reads=["o_hf" + n_, "o_hbk" + n_], writes=["o_hf" + n_])
                s.op("act", lambda e: e.activation(out=sqh[:], in_=hf[a][:], func=AF.Square),
                     reads=["o_hf" + n_], writes=["o_sq"])
                s.op("dve", lambda e: e.tensor_reduce(out=ms[a][:], in_=sqh[:].rearrange("p (h e) -> p h e", h=8),
                                                      axis=mybir.AxisListType.X, op=ALU.add),
                     reads=["o_sq"], writes=["o_ms" + n_])
                s.op("act", lambda e: e.activation(out=ms[a][:], in_=ms[a][:], func=AF.Sqrt, scale=1.0 / 256, bias=EPS),
                     reads=["o_ms" + n_], writes=["o_ms" + n_])
                s.op("dve", lambda e: e.reciprocal(out=ms[a][:], in_=ms[a][:]), reads=["o_ms" + n_], writes=["o_ms" + n_])
                s.op("pool", lambda e: e.tensor_tensor(out=hbk[a][:], in0=so[a][:], in1=gh[:], op=ALU.mult),
                     reads=["o_so" + n_, "o_gh", "o_hbk" + n_], writes=["o_hbk" + n_])
                for h in range(8):
                    s.op("dve", lambda e, h=h: e.scalar_tensor_tensor(
                        out=yb[a][:, h * 256:(h + 1) * 256], in0=hf[a][:, h * 256:(h + 1) * 256], scalar=ms[a][:, h:h + 1],
                        in1=hbk[a][:, h * 256:(h + 1) * 256], op0=ALU.mult, op1=ALU.mult),
                        reads=["o_hf" + n_, "o_hbk" + n_, "o_ms" + n_], writes=[("o_yb" + n_, h, h + 1)])
            def m4_tr(b):
                a = b % 2
                n_ = "%d" % a
                for q in range(2):
                    pb = (b * 2 + q) % 2
                    for r in range(8):
                        kc = q * 8 + r
                        s.op("pe", lambda e, kc=kc, r=r, pb=pb: e.transpose(
                            psb16[pb][:, r * 128:(r + 1) * 128], yb[a][:, kc * 128:(kc + 1) * 128], idb[:]),
                            reads=["o_yb" + n_, "o_idb"], writes=["ps%d" % pb])
                    if q == 0:
                        s.op("act", lambda e, pb=pb: e.activation(
                            out=yT[a][:, 0:8, :].rearrange("p k t -> p (k t)"), in_=psb16[pb][:, :], func=AF.Copy),
                            reads=["ps%d" % pb], writes=[("o_yT" + n_, 0, 8)])
                    else:
                        s.op("dve", lambda e, pb=pb: e.tensor_copy(
                            out=yT[a][:, 8:16, :].rearrange("p k t -> p (k t)"), in_=psb16[pb][:, :]),
                            reads=["ps%d" % pb], writes=[("o_yT" + n_, 8, 16)])
            def m4_back(b):
                a = b % 2
                t0 = b * 128
                sidx = 1 if t0 < CT else 0
                n_ = "%d" % a
                self.dma("sp", xr[a][:], XTv[:, :, t0:t0 + 128], reads=[("XT", t0, t0 + 128)], writes=["o_xr" + n_])
                for q4 in range(4):
                    pb = 2 + (b * 4 + q4) % 4
                    for r in range(4):
                        dc = q4 * 4 + r
                        for kc in range(KC):
                            s.op("pe", lambda e, dc=dc, kc=kc, r=r, pb=pb: e.matmul(
                                self.ps[pb][:, r * 128:(r + 1) * 128], lhsT=mwo[:, kc, dc * 128:(dc + 1) * 128],
                                rhs=yT[a][:, kc, :], start=(kc == 0), stop=(kc == KC - 1)),
                                reads=["o_wo", "o_yT" + n_], writes=["ps%d" % pb])
                    for r in range(4):
                        dc = q4 * 4 + r
                        s.op("dve", lambda e, dc=dc, r=r, pb=pb: e.scalar_tensor_tensor(
                            out=xo[a][:, dc, :], in0=self.ps[pb][:, r * 128:(r + 1) * 128],
                            scalar=self.MOD[:, l, 32 + dc, sidx:sidx + 1], in1=xr[a][:, dc, :],
                            op0=ALU.mult, op1=ALU.add),
                            reads=["ps%d" % pb, "o_xr" + n_], writes=[("o_xo" + n_, dc, dc + 1)])
                self.dma("sp", XTv[:, :, t0:t0 + 128], xo[a][:], reads=["o_xo" + n_], writes=[("XT", t0, t0 + 128)])

            last_full = self.plan_is_full() and l == self.depth - 1
            blks = [b for b in range(NB) if not (last_full and b < CT // 128)]
            nbk = len(blks)
            for i in range(min(2, nbk)):
                m4_block(blks[i])
                m4_tr(blks[i])
            for i in range(nbk):
                if i + 2 < nbk:
                    m4_block(blks[i + 2])
                m4_back(blks[i])
                if i + 2 < nbk:
                    m4_tr(blks[i + 2])
        s.barrier()

    def plan_is_full(self):
        return getattr(self, "full", False)

    def final(self):
        nc, s = self.nc, self.s
        L = self.L
        XTv = self.XTv()
        if self.xt_dbg is not None:
            self.dma("sp", self.xt_dbg.ap(), self.XT.ap(), reads=["XT"], writes=["xt_dbg"])
        with ExitStack() as st:
            xt = [self.sb(st, "z_xt%d" % i, [128, KC, 128], F32) for i in range(2)]
            sq = [self.sb(st, "z_sq%d" % i, [128, KC, 128], BF16) for i in range(2)]
            scr = [self.sb(st, "z_scr%d" % i, [128, 128], F32) for i in range(2)]
            yo = [self.sb(st, "z_yo%d" % i, [128, D], F32) for i in range(2)]
            def fin_front(bi):
                a = bi % 2
                t0 = CT + bi * 128
                xn, qn, sn, yn = "z_xt%d" % a, "z_sq%d" % a, "z_scr%d" % a, "z_yo%d" % a
                self.dma("sp", xt[a][:], XTv[:, :, t0:t0 + 128], reads=[("XT", t0, t0 + 128)], writes=[xn])
                s.op("act", lambda e, a=a: e.activation(out=sq[a][:], in_=xt[a][:], func=AF.Square),
                     reads=[xn], writes=[qn])
                for kc in range(KC):
                    s.op("pe", lambda e, a=a, kc=kc: e.matmul(self.ps[6][:, 0:128], lhsT=self.ones_b[:],
                                                              rhs=sq[a][:, kc, :], start=(kc == 0),
                                                              stop=(kc == KC - 1)),
                         reads=[qn, "ones_b"], writes=["ps6"])
                s.op("act", lambda e, a=a: e.activation(out=scr[a][:], in_=self.ps[6][:, 0:128], func=AF.Sqrt,
                                                        scale=1.0 / D, bias=EPS), reads=["ps6"], writes=[sn])
                s.op("dve", lambda e, a=a: e.reciprocal(out=scr[a][:], in_=scr[a][:]), reads=[sn], writes=[sn])
                for kc in range(KC):
                    me = "dve" if kc % 2 == 0 else "pool"
                    if me == "dve":
                        s.op("dve", lambda e, a=a, kc=kc: e.scalar_tensor_tensor(
                            out=xt[a][:, kc, :], in0=xt[a][:, kc, :], scalar=self.gfin[:, kc:kc + 1],
                            in1=scr[a][:], op0=ALU.mult, op1=ALU.mult),
                            reads=[(xn, kc, kc + 1), sn, "gfin"], writes=[(xn, kc, kc + 1)])
                    else:
                        s.op("pool", lambda e, a=a, kc=kc: e.tensor_tensor(
                            out=xt[a][:, kc, :], in0=xt[a][:, kc, :], in1=scr[a][:], op=ALU.mult),
                            reads=[(xn, kc, kc + 1), sn], writes=[(xn, kc, kc + 1)])
                        s.op("pool", lambda e, a=a, kc=kc: e.tensor_scalar(
                            out=xt[a][:, kc, :], in0=xt[a][:, kc, :], scalar1=self.gfin[:, kc:kc + 1], scalar2=1.0,
                            op0=ALU.mult, op1=ALU.mult),
                            reads=[(xn, kc, kc + 1), "gfin"], writes=[(xn, kc, kc + 1)])
            def fin_back(bi):
                a = bi % 2
                xn, qn, sn, yn = "z_xt%d" % a, "z_sq%d" % a, "z_scr%d" % a, "z_yo%d" % a
                for q in range(4):
                    pb = (bi * 4 + q) % 6
                    for r in range(4):
                        kc = q * 4 + r
                        s.op("pe", lambda e, a=a, kc=kc, pb=pb, r=r: e.transpose(
                            self.ps[pb][:, r * 128:(r + 1) * 128], xt[a][:, kc, :], self.ident[:]),
                            reads=[(xn, kc, kc + 1), "ident"], writes=["ps%d" % pb])
                    if q % 2 == 0:
                        s.op("act", lambda e, a=a, q=q, pb=pb: e.activation(
                            out=yo[a][:, q * 512:(q + 1) * 512], in_=self.ps[pb][:, :], func=AF.Copy),
                            reads=["ps%d" % pb], writes=[(yn, q, q + 1)])
                    else:
                        s.op("dve", lambda e, a=a, q=q, pb=pb: e.tensor_copy(
                            out=yo[a][:, q * 512:(q + 1) * 512], in_=self.ps[pb][:, :]),
                            reads=["ps%d" % pb], writes=[(yn, q, q + 1)])
                self.dma("sp", self.out.ap()[bi * 128:(bi + 1) * 128, :], yo[a][:], reads=[yn], writes=["out"])

            nb_ = L // 128
            for bi in range(nb_):
                fin_front(bi)
                if bi > 0:
                    fin_back(bi - 1)
            fin_back(nb_ - 1)
        s.barrier()


def lay_pk(v):
    sh = v.shape[:-1]
    return np.ascontiguousarray(np.swapaxes(v.reshape(sh + (-1, 128)), -1, -2))


def host_inputs(inp, b, depth):
    f = np.float32
    m = {}
    m["x"] = np.ascontiguousarray(inp["x"][b], dtype=f)
    m["ctx"] = np.ascontiguousarray(inp["ctx"][b], dtype=f)
    m["cc"] = np.ascontiguousarray(np.stack([lay_pk(inp["c"][b]), lay_pk(inp["c_ctx"])], axis=-1), dtype=f)
    bm = lay_pk(inp["b_mod"][:depth])
    m["b_mod"] = np.ascontiguousarray(np.repeat(bm[..., None], 2, axis=-1))
    m["g_mix"] = np.ascontiguousarray(np.repeat(lay_pk(inp["g_mix"][:depth])[..., None], 2, axis=-1))
    m["g_ffn"] = np.ascontiguousarray(np.repeat(lay_pk(inp["g_ffn"][:depth])[..., None], 2, axis=-1))
    m["g_final"] = lay_pk(inp["g_final"])
    m["conv_w"] = np.ascontiguousarray(np.transpose(lay_pk(inp["ffn_conv_w"][:depth]), (0, 2, 3, 1)))
    m["conv_b"] = lay_pk(inp["ffn_conv_b"][:depth])
    m["ident"] = np.eye(128, dtype=f)
    na = (depth + 1) // 2
    L = m["x"].shape[0]
    T = CT + L
    m["w_qkv"] = inp["attn_w_qkv"][:max(na, 1)]
    m["w_o"] = inp["attn_w_o"][:max(na, 1)]
    sk = inp["attn_sink"][:max(na, 1)]
    m["sink"] = np.ascontiguousarray(np.broadcast_to(sk[:, None, :, None], (sk.shape[0], 128, 16, 128)), dtype=f)
    nm_ = depth // 2
    m["w_in"] = inp["mlstm_w_in"][:max(nm_, 1)]
    m["m_w_o"] = inp["mlstm_w_o"][:max(nm_, 1)]
    bi = inp["mlstm_b_in"][:max(nm_, 1)]
    m["m_brow"] = np.ascontiguousarray(np.broadcast_to(bi[:, None, 1024:6176], (bi.shape[0], 128, 5152)), dtype=f)
    m["m_bqk"] = lay_pk(bi[:, 0:2048])
    gh_ = inp["mlstm_g_head"][:max(nm_, 1)]
    m["g_head"] = np.ascontiguousarray(np.broadcast_to(gh_[:, None, :], (gh_.shape[0], 128, D)), dtype=f)
    ss_ = np.arange(128)[:, None]
    tt_ = np.arange(128)[None, :]
    m["tri"] = np.ascontiguousarray(np.stack([(ss_ <= tt_), (ss_ >= tt_)], axis=1).astype(f))
    tok = np.arange(L)
    row = (tok // 64).astype(np.float64)
    col = (tok % 64).astype(np.float64)
    inv = 10000.0 ** (-np.arange(32, dtype=np.float64) / 32)
    ang = np.concatenate([row[:, None] * inv, col[:, None] * inv], axis=-1)
    ang = np.concatenate([ang, ang], axis=-1).astype(f)
    cosT = np.ones((128, T), f)
    sinT = np.zeros((128, T), f)
    cosT[:, CT:] = np.cos(ang).T
    sinT[:, CT:] = np.sin(ang).T
    m["cosT"] = cosT
    m["sinT"] = sinT
    rp = np.zeros((128, 128), f)
    for do in range(128):
        if do < 64:
            rp[do + 64, do] = -1.0
        else:
            rp[do - 64, do] = 1.0
    m["rperm"] = rp
    kj = np.arange(128)[:, None]
    qi = np.arange(128)[None, :]
    mprev = np.where(kj >= qi, 0.0, -1e30).astype(f)
    mnext = np.where(kj <= qi, 0.0, -1e30).astype(f)
    m["mask"] = np.ascontiguousarray(np.stack([np.tile(mprev, (1, 4)), np.tile(mnext, (1, 4))], axis=1))
    return m


_WCACHE = {}


def host_weights(inp, depth):
    wup = inp["ffn_w_up"][:depth]
    w = wup.reshape(depth, KC, 128, 2, NJ, 128)
    w = np.transpose(w, (0, 4, 2, 1, 3, 5))
    wup_b = np.ascontiguousarray(w).reshape(depth, NJ * 128, KC * 256)
    wdn = inp["ffn_w_down"][:depth]
    w = wdn.reshape(depth, NJ, 128, 16, 128)
    w = np.transpose(w, (0, 3, 2, 1, 4))
    wdn_b = np.ascontiguousarray(w).reshape(depth, 16 * 128, NJ * 128)
    w = inp["w_mod"][:depth].reshape(depth, KC, 128, 24, 512)
    wmod_b = np.ascontiguousarray(np.transpose(w, (0, 3, 2, 1, 4))).reshape(depth, 24 * 128, KC * 512)
    return {"w_up": wup_b, "w_down": wdn_b, "w_mod": wmod_b}


def kernel(**inputs):
    depth = 4
    L = 4096
    plan = [("att", 0), ("ffn", 0), ("mls", 1), ("ffn", 1), ("att", 2), ("ffn", 2), ("mls", 3), ("ffn", 3)]
    inputs = {k: np.asarray(v) for k, v in inputs.items()}
    P = Prog(L, plan, depth)
    P.full = True
    nc = P.build()
    shared = host_weights(inputs, depth)
    in_maps = []
    for b in range(8):
        m = host_inputs(inputs, b, depth)
        m.update(shared)
        in_maps.append(m)
    res = run_bass_kernel_spmd(nc, in_maps, core_ids=list(range(8)))
    return np.stack([np.asarray(r["out"], dtype=np.float32) for r in res.results], axis=0)
```
